# Optimizing a Trainium2 kernel written in Bass

```python
import jax, jax.numpy as jnp
from jax import lax
import numpy as np

D_MODEL = 2048
BATCH = 2
SEQ = 4096
DEPTH = 4

CHUNK = 128
SGU_GROUPS = 8
SGU_GROUP_DIM = 128
SGU_WIDTH = SGU_GROUPS * SGU_GROUP_DIM
RET_HEADS = 8
RET_DK = 128
RET_DV = 256
RET_QK_WIDTH = RET_HEADS * RET_DK
RET_V_WIDTH = RET_HEADS * RET_DV
ROPE_THETA = 10000.0
D_FF = 5632
CONV_W = 3
PLE_DIM = 256
NORM_EPS = 1e-6
IN_SIZES = (SGU_WIDTH, SGU_WIDTH, RET_QK_WIDTH, RET_QK_WIDTH, RET_V_WIDTH, RET_V_WIDTH, D_MODEL, D_MODEL)
IN_COLS = sum(IN_SIZES)
IN_OFFSETS = tuple(int(o) for o in np.cumsum(IN_SIZES)[:-1])

kernel_name = "hybrid_sgu_retention_convffn_ple"


def rms_norm(x, g):
    xf = x.astype(jnp.float32)
    y = xf * lax.rsqrt(jnp.mean(xf * xf, axis=-1, keepdims=True) + NORM_EPS)
    return (y * g.astype(jnp.float32)).astype(x.dtype)


def rotary(x, pos):
    half = x.shape[-1] // 2
    inv = ROPE_THETA ** (-jnp.arange(half, dtype=jnp.float32) / half)
    ang = pos.astype(jnp.float32)[:, None] * inv[None, :]
    cos = jnp.cos(ang)[None, :, None, :].astype(x.dtype)
    sin = jnp.sin(ang)[None, :, None, :].astype(x.dtype)
    x1, x2 = x[..., :half], x[..., half:]
    return jnp.concatenate([x1 * cos - x2 * sin, x2 * cos + x1 * sin], axis=-1)


def spatial_gating(u, v, norm_g, w_s, b_s):
    B, S, _ = v.shape
    n = S // CHUNK
    v = rms_norm(v, norm_g)
    vg = v.reshape(B, n, CHUNK, SGU_GROUPS, SGU_GROUP_DIM)
    mask = jnp.tril(jnp.ones((CHUNK, CHUNK), dtype=w_s.dtype))
    w = w_s * mask[None]
    s = jnp.einsum('gts,bnsgd->bntgd', w, vg) + b_s.T[:, :, None]
    return u * s.reshape(B, S, SGU_WIDTH)


def retention(q, k, v):
    B, S, H, DK = q.shape
    DV = v.shape[-1]
    n = S // CHUNK
    q = q.reshape(B, n, CHUNK, H, DK)
    k = k.reshape(B, n, CHUNK, H, DK)
    v = v.reshape(B, n, CHUNK, H, DV)
    log_g = jnp.log1p(-jnp.exp2(-5.0 - jnp.arange(H, dtype=jnp.float32)))
    idx = jnp.arange(CHUNK, dtype=jnp.float32)
    diff = idx[:, None] - idx[None, :]
    causal = diff >= 0
    decay = jnp.where(causal[None], jnp.exp(log_g[:, None, None] * jnp.where(causal, diff, 0.0)[None]), 0.0)
    xi = jnp.exp(log_g[:, None] * (idx[None, :] + 1.0)).T
    zeta = jnp.exp(log_g[:, None] * (CHUNK - 1.0 - idx[None, :])).T
    chunk_decay = jnp.exp(log_g * CHUNK)
    scores = jnp.einsum('bnthd,bnshd->bnhts', q, k) * decay.astype(q.dtype)[None, None]
    inner = jnp.einsum('bnhts,bnshe->bnthe', scores, v)
    kv = jnp.einsum('bnshd,bnshe->bnhde', k * zeta.astype(k.dtype)[None, None, :, :, None], v)
    cd = chunk_decay.astype(kv.dtype)[None, :, None, None]

    def step(state, kv_n):
        return state * cd + kv_n, state

    _, prev = lax.scan(step, jnp.zeros((B, H, DK, DV), kv.dtype), jnp.moveaxis(kv, 1, 0))
    prev = jnp.moveaxis(prev, 0, 1)
    cross = jnp.einsum('bnthd,bnhde->bnthe', q, prev) * xi.astype(q.dtype)[None, None, :, :, None]
    return (inner + cross).reshape(B, S, H, DV)


def causal_dwconv(a, w, b):
    S = a.shape[1]
    K = w.shape[0]
    ap = jnp.pad(a, ((0, 0), (K - 1, 0), (0, 0)))
    out = ap[:, 0:S] * w[0]
    for j in range(1, K):
        out = out + ap[:, j:j + S] * w[j]
    return out + b


def setup_inputs(seed: int = 0) -> dict:
    key = jax.random.key(seed)
    ks = jax.random.split(key, 22)
    f32 = jnp.float32

    def nrm(k, shape, scale):
        return jax.random.normal(k, shape, f32) * scale

    def gain(k, shape):
        return 1.0 + 0.02 * jax.random.normal(k, shape, f32)

    return {
        "x": nrm(ks[0], (BATCH, SEQ, D_MODEL), 1.0),
        "p": nrm(ks[1], (DEPTH, BATCH, SEQ, PLE_DIM), 1.0),
        "mix_norm_g": gain(ks[2], (DEPTH, D_MODEL)),
        "w_in": nrm(ks[3], (DEPTH, D_MODEL, IN_COLS), D_MODEL ** -0.5),
        "sgu_norm_g": gain(ks[4], (DEPTH, SGU_WIDTH)),
        "sgu_w": nrm(ks[5], (DEPTH, SGU_GROUPS, CHUNK, CHUNK), CHUNK ** -0.5),
        "sgu_b": nrm(ks[6], (DEPTH, SGU_GROUPS, CHUNK), 0.02),
        "ret_norm_g": gain(ks[7], (DEPTH, RET_V_WIDTH)),
        "w_branch_a": nrm(ks[8], (DEPTH, SGU_WIDTH, D_MODEL), SGU_WIDTH ** -0.5),
        "w_branch_b": nrm(ks[9], (DEPTH, RET_V_WIDTH, D_MODEL), RET_V_WIDTH ** -0.5),
        "w_out": nrm(ks[10], (DEPTH, D_MODEL, D_MODEL), D_MODEL ** -0.5),
        "ffn_norm_g": gain(ks[11], (DEPTH, D_MODEL)),
        "ffn_w_gate": nrm(ks[12], (DEPTH, D_MODEL, D_FF), D_MODEL ** -0.5),
        "ffn_w_up": nrm(ks[13], (DEPTH, D_MODEL, D_FF), D_MODEL ** -0.5),
        "ffn_conv_w": nrm(ks[14], (DEPTH, CONV_W, D_FF), CONV_W ** -0.5),
        "ffn_conv_b": nrm(ks[15], (DEPTH, D_FF), 0.02),
        "ffn_w_down": nrm(ks[16], (DEPTH, D_FF, D_MODEL), D_FF ** -0.5),
        "ple_norm_g": gain(ks[17], (DEPTH, D_MODEL)),
        "ple_w_gate": nrm(ks[18], (DEPTH, D_MODEL, D_MODEL), D_MODEL ** -0.5),
        "ple_w_proj": nrm(ks[19], (DEPTH, PLE_DIM, D_MODEL), PLE_DIM ** -0.5),
        "final_norm_g": gain(ks[20], (D_MODEL,)),
    }


def reference(x, p, mix_norm_g, w_in, sgu_norm_g, sgu_w, sgu_b, ret_norm_g, w_branch_a, w_branch_b,
              w_out, ffn_norm_g, ffn_w_gate, ffn_w_up, ffn_conv_w, ffn_conv_b, ffn_w_down,
              ple_norm_g, ple_w_gate, ple_w_proj, final_norm_g):
    B, S, _ = x.shape
    pos = jnp.arange(S, dtype=jnp.int32)
    for i in range(DEPTH):
        h = rms_norm(x, mix_norm_g[i])
        z = h @ w_in[i]
        u, v, q, k, rv, rg, ga, gb = jnp.split(z, IN_OFFSETS, axis=-1)
        ya = spatial_gating(jax.nn.gelu(u), jax.nn.gelu(v), sgu_norm_g[i], sgu_w[i], sgu_b[i]) @ w_branch_a[i]
        q = rotary(q.reshape(B, S, RET_HEADS, RET_DK), pos)
        k = rotary(k.reshape(B, S, RET_HEADS, RET_DK), pos) * (RET_DK ** -0.5)
        y = retention(q, k, rv.reshape(B, S, RET_HEADS, RET_DV))
        y = rms_norm(y, ret_norm_g[i].reshape(RET_HEADS, RET_DV)).reshape(B, S, RET_V_WIDTH)
        yb = (jax.nn.silu(rg) * y) @ w_branch_b[i]
        m = jax.nn.sigmoid(ga) * ya + jax.nn.sigmoid(gb) * yb
        x = x + m @ w_out[i]
        h = rms_norm(x, ffn_norm_g[i])
        a = causal_dwconv(h @ ffn_w_gate[i], ffn_conv_w[i], ffn_conv_b[i])
        x = x + (jax.nn.gelu(a) * (h @ ffn_w_up[i])) @ ffn_w_down[i]
        g = jax.nn.sigmoid(rms_norm(x, ple_norm_g[i]) @ ple_w_gate[i])
        x = x + (p[i] @ ple_w_proj[i]) * g
    return rms_norm(x, final_norm_g)
```

```python
import math
from contextlib import ExitStack

import numpy as np
import concourse.bass as bass
import concourse.mybir as mybir
from concourse.bass_utils import run_bass_kernel_spmd

F32 = mybir.dt.float32
BF16 = mybir.dt.bfloat16
AF = mybir.ActivationFunctionType
ALU = mybir.AluOpType

NL = 4
D = 2048
TT = 512
NCH = 4
DFF = 5632
NF = 44
FG = 4
NFG = 11
NSLOT = 8
EPS = 1e-6
H = 8
SL_U, SL_V, SL_Q, SL_K, SL_RV, SL_RG, SL_GA, SL_GB = 0, 8, 16, 24, 32, 48, 64, 80

C_MIX, C_FFN, C_PLE, C_RET, C_CW, C_CB = 0, 16, 32, 48, 64, 196
C_PER_LAYER = 240
C_FINAL = NL * C_PER_LAYER
C_ZETA = C_FINAL + 16
C_CDN = C_ZETA + 8
C_COEF = C_CDN + 32
C_SEL = C_COEF + 112
C_EPS = C_SEL + 16
C_TOTAL = C_EPS + 1


def layer_plan():
    plan = []
    for s in range(8):
        plan.append(("in", SL_K + s))
    for s in range(16):
        plan.append(("in", SL_RV + s))
    for s in range(8):
        plan.append(("in", SL_Q + s))
    for s in range(8):
        plan.append(("in", SL_V + s))
    for s in range(8):
        plan.append(("in", SL_U + s))
    for s in range(16):
        plan.append(("in", SL_RG + s))
    for o in range(16):
        plan.append(("in", SL_GA + o))
        if o % 2 == 0:
            plan.append(("wa", o // 2))
        plan.append(("in", SL_GB + o))
        plan.append(("wb", o))
    for o in range(16):
        plan.append(("wout", o))
    for fg in range(NFG):
        for f in range(FG):
            plan.append(("gate", fg * FG + f))
            plan.append(("up", fg * FG + f))
        for q in range(4):
            plan.append(("down", fg, q))
    for o in range(16):
        plan.append(("pg", o))
        if o % 8 == 0:
            plan.append(("pp", o // 8))
    assert len(plan) == 286
    plan.append(None)
    plan.append(None)
    return plan


PLAN = layer_plan()
NREAL = 286
LAYERS = (0, 1, 2, 3)


class Buf:
    __slots__ = ("w", "r")

    def __init__(self):
        self.w = {}
        self.r = {}


class EngState:
    def __init__(self, name):
        self.name = name
        self.ops = []
        self.sem = None
        self.count = 0
        self.seen = {}


class Tracker:
    ROT = 3000

    def __init__(self, nc, es):
        self.nc = nc
        self.es = es
        self.nsem = 0
        self.sems = {}
        self.engs = {n: EngState(n) for n in ("pe", "act", "dve", "pool", "sp")}
        for e in self.engs.values():
            e.sem = self.new_sem(e.name)
        self.misc = {q: [[self.new_sem("m" + q), 0] for _ in range(4)] for q in ("sp", "pool")}
        self.misc_i = {"sp": 0, "pool": 0}

    def new_sem(self, tag):
        self.nsem += 1
        s = self.es.enter_context(self.nc.semaphore(f"{tag}_{self.nsem}"))
        self.sems[id(s)] = s
        return s

    def _deps(self, reads, writes):
        deps = {}
        for b in reads:
            for k, v in b.w.items():
                if deps.get(k, 0) < v:
                    deps[k] = v
        for b in writes:
            for d in (b.w, b.r):
                for k, v in d.items():
                    if deps.get(k, 0) < v:
                        deps[k] = v
        return deps

    def _waits(self, e, deps):
        for k, v in deps.items():
            if e.seen.get(k, 0) >= v:
                continue
            if k == id(e.sem) and e.name in ("pe", "pool", "sp"):
                continue
            e.seen[k] = v
            sem = self.sems[k]
            e.ops.append(lambda eng, sem=sem, v=v: eng.wait_ge(sem, v))

    def _mark(self, tok, reads, writes):
        k, v = tok
        for b in reads:
            if b.r.get(k, 0) < v:
                b.r[k] = v
        for b in writes:
            if b.r:
                b.w = {k: v}
                b.r = {}
            else:
                b.w[k] = v

    def emit(self, en, fn, reads=(), writes=()):
        e = self.engs[en]
        self._waits(e, self._deps(reads, writes))
        if e.count >= self.ROT:
            e.sem = self.new_sem(e.name)
            e.count = 0
        e.count += 1
        sem, cnt = e.sem, e.count
        e.ops.append(lambda eng, fn=fn, sem=sem: fn(eng).then_inc(sem, 1))
        self._mark((id(sem), cnt), reads, writes)

    def dma(self, en, out, in_, reads=(), writes=(), semslot=None):
        e = self.engs[en]
        self._waits(e, self._deps(reads, writes))
        if semslot is None:
            semslot = self.misc[en][self.misc_i[en]]
            self.misc_i[en] = (self.misc_i[en] + 1) % len(self.misc[en])
        sem, val = semslot
        if val > 0:
            self._waits(e, {id(sem): val})
        if val >= 16 * 180:
            sem = self.new_sem("d")
            semslot[0] = sem
            val = 0
        val += 16
        semslot[1] = val
        e.ops.append(lambda eng, out=out, in_=in_, sem=sem: eng.dma_start(out=out, in_=in_).then_inc(sem, 16))
        self._mark((id(sem), val), reads, writes)

    def wait_all(self, en, bufs):
        e = self.engs[en]
        self._waits(e, self._deps((), bufs))


class _Stop(Exception):
    pass


def build_program(layers=(0, 1, 2, 3), do_final=True, tiles=(0, 1), stop=None):
    nc = bass.Bass("TRN2", target_bir_lowering=False)
    es = ExitStack()
    with es:
        T = Tracker(nc, es)
        xin = nc.dram_tensor("xin", [2, 128, 16, TT], F32, kind="ExternalInput")
        pin = nc.dram_tensor("pin", [NL, 2, 128, 2, TT], F32, kind="ExternalInput")
        wst = nc.dram_tensor("wst", [len(layers), NREAL, 128, 2048], F32, kind="ExternalInput")
        colsd = nc.dram_tensor("cols", [128, C_TOTAL], F32, kind="ExternalInput")
        sgug = nc.dram_tensor("sgug", [NL, 1, 1024], F32, kind="ExternalInput")
        sgub = nc.dram_tensor("sgub", [NL, 1, 1024], F32, kind="ExternalInput")
        sguw = nc.dram_tensor("sguw", [NL, 128, 1024], F32, kind="ExternalInput")
        tabs = nc.dram_tensor("tabs", [128, 2048], F32, kind="ExternalInput")
        rott = nc.dram_tensor("rott", [2, 128, 3, NCH, 64], F32, kind="ExternalInput")
        cbf = nc.dram_tensor("cbf", [128, 384], F32, kind="ExternalInput")
        yout = nc.dram_tensor("yout", [2, 128, 16, TT], F32, kind="ExternalOutput")
        agL_in = [[nc.dram_tensor(f"agLi_{t}_{l}", [128, 2048], F32) for l in range(NL)] for t in range(2)]
        agL_out = [[nc.dram_tensor(f"agLo_{t}_{l}", [512, 2048], F32) for l in range(NL)] for t in range(2)]
        agH_in = [[nc.dram_tensor(f"agHi_{t}_{l}", [128, 32], BF16) for l in range(NL)] for t in range(2)]
        agH_out = [[nc.dram_tensor(f"agHo_{t}_{l}", [512, 32], BF16) for l in range(NL)] for t in range(2)]

        def sb(name, shape, dt):
            return es.enter_context(nc.sbuf_tensor(name, shape, dt))

        xT = sb("xT", [128, 16, TT], F32)
        hT = sb("hT", [128, 16, TT], BF16)
        ring = sb("ring", [128, NSLOT, 2048], BF16)
        QK = sb("QK", [128, 2, NCH, 1024], BF16)
        RV = sb("RV", [128, NCH, 2048], BF16)
        PN = sb("PN", [128, 3, 2048], BF16)
        uT = sb("uT", [128, 8, TT], BF16)
        ST = sb("ST", [128, 2048], F32)
        SCR = sb("SCR", [128, 24 * 1024], mybir.dt.uint8)
        SQ = sb("SQ", [128, 2, 4, TT], BF16)
        rstd = sb("rstd", [128, TT], F32)
        cols = sb("colsS", [128, C_TOTAL], F32)
        decT = sb("decT", [128, 8, 128], F32)
        xi0 = sb("xi0", [128, 8, 128], F32)
        rot = sb("rot", [128, 3, NCH, 64], F32)
        gbc = sb("gbc", [128, 1024], F32)
        WmT = sb("WmT", [128, 8, 128], BF16)
        sgb = sb("sgb", [1, 1024], BF16)
        cb = sb("cbS", [128, 384], BF16)
        pT = sb("pT", [128, 2, TT], BF16)
        hal = sb("hal", [128, 2, 4, 32], BF16)
        hacc = sb("hacc", [128, 32], F32)
        hh = sb("hh", [128, 16, 2], BF16)
        hsend = sb("hsend", [128, 16, 2], BF16)
        ssq = sb("ssq", [128, 8], F32)
        psA = es.enter_context(nc.psum_tensor("psA", [128, 6, 512], F32))
        psB = es.enter_context(nc.psum_tensor("psB", [128, 1024], F32))

        ones = cb[:, 0:128]
        ident = cb[:, 128:256]
        maskT = cb[:, 256:384]

        def scr(off_kib, shape, dt):
            nbytes = int(np.prod(shape[1:])) * (2 if dt == BF16 else 4)
            v = SCR[:, off_kib * 1024: off_kib * 1024 + nbytes].bitcast(dt)
            if len(shape) == 3:
                return v.rearrange("p (a b) -> p a b", b=shape[2])
            if len(shape) == 4:
                return v.rearrange("p (a b c) -> p a b c", b=shape[2], c=shape[3])
            return v

        b_x = [Buf() for _ in range(16)]
        b_h = [Buf() for _ in range(16)]
        b_slot = [Buf() for _ in range(NSLOT)]
        b_q = [Buf() for _ in range(NCH)]
        b_k = [Buf() for _ in range(NCH)]
        b_rv = [Buf() for _ in range(NCH)]
        b_pn = [Buf() for _ in range(3)]
        b_u = [Buf() for _ in range(8)]
        b_st = Buf()
        b_scr = [Buf() for _ in range(24)]
        b_sq = [Buf(), Buf()]
        b_rstd = Buf()
        b_const = Buf()
        b_layer = Buf()
        b_rot = Buf()
        b_pT = Buf()
        b_hal = Buf()
        b_hacc = Buf()
        b_hh = Buf()
        b_hsend = Buf()
        b_hp = [Buf() for _ in range(8)]
        b_ssq = Buf()
        b_psA = [Buf() for _ in range(6)]
        b_psBk = [Buf(), Buf()]
        b_psB = b_psBk[0]
        b_agL = [[Buf() for _ in range(NL)] for _ in range(2)]
        b_agH = [[Buf() for _ in range(NL)] for _ in range(2)]
        b_out = Buf()

        def sbufs(off, n):
            return b_scr[off:off + n]

        pa = [0]

        def bank(n=1):
            p = pa[0]
            if n > 1 and p % n:
                p += n - p % n
            if p + n > 6:
                p = 0
            pa[0] = (p + n) % 6
            return p

        slot_sems = [[T.new_sem("w"), 0] for _ in range(NSLOT)]
        rs = {"issued": 0, "consumed": 0, "seq": [], "released": set(), "slot_req": {}}

        def ring_pump():
            while rs["issued"] < len(rs["seq"]):
                r = rs["issued"]
                if r >= NSLOT and (r - NSLOT) not in rs["released"]:
                    break
                ent = rs["seq"][r]
                s = r % NSLOT
                if ent is not None:
                    li, idx = ent
                    T.dma("pool", ring[:, s, :], wst[li, idx, :, :], reads=(), writes=(b_slot[s],),
                          semslot=slot_sems[s])
                rs["issued"] += 1

        def ring_next(desc):
            r = rs["consumed"]
            assert PLAN[r % len(PLAN)] == desc, (r, PLAN[r % len(PLAN)], desc)
            rs["consumed"] += 1
            if desc is None:
                rs["released"].add(r)
                ring_pump()
                return None
            ring_pump()
            assert rs["issued"] > r, (r, rs["issued"])
            s = r % NSLOT
            rs["slot_req"][s] = r
            return s

        def ring_release(s):
            rs["released"].add(rs["slot_req"][s])
            ring_pump()

        real_idx = {}
        ri = 0
        for i, d_ in enumerate(PLAN):
            if d_ is not None:
                real_idx[i] = ri
                ri += 1
        for tt in tiles:
            for li in layers:
                for i, d_ in enumerate(PLAN):
                    rs["seq"].append(None if d_ is None else (layers.index(li), real_idx[i]))

        def act(out, in_, func, reads, writes, bias=None, scale=None, accum_out=None):
            kw = {}
            if bias is not None:
                kw["bias"] = bias
            if scale is not None:
                kw["scale"] = scale
            if accum_out is not None:
                kw["accum_out"] = accum_out
            T.emit("act", lambda e: e.activation(out=out, in_=in_, func=func, **kw), reads, writes)

        def tt_(out, a, b, op, reads, writes):
            T.emit("dve", lambda e: e.tensor_tensor(out=out, in0=a, in1=b, op=op), reads, writes)

        def ts_(out, a, s1, s2, op0, op1, reads, writes):
            if s2 is None:
                s2, op1 = 0.0, ALU.add
            T.emit("dve", lambda e: e.tensor_scalar(out=out, in0=a, scalar1=s1, scalar2=s2, op0=op0, op1=op1),
                   reads, writes)

        def rsqrt_(out, in_, inv_n, reads_in, wbufs):
            act(out, in_, AF.Sqrt, reads=list(reads_in) + [b_const], writes=wbufs, bias=col(C_EPS), scale=inv_n)
            T.emit("dve", lambda e: e.reciprocal(out=out, in_=out), reads=wbufs, writes=wbufs)

        def stt(out, in0, scalar, in1, op0, op1, reads, writes):
            T.emit("dve", lambda e: e.scalar_tensor_tensor(out=out, in0=in0, scalar=scalar, in1=in1, op0=op0, op1=op1),
                   reads, writes)

        def mm(outs_pairs, reads, writes):
            def fn(pe):
                ins = None
                for out_ap, pairs in outs_pairs:
                    n = len(pairs)
                    for i, (l, r) in enumerate(pairs):
                        ins = pe.matmul(out_ap, l, r, start=(i == 0), stop=(i == n - 1))
                return ins
            T.emit("pe", fn, reads, writes)

        T.dma("sp", cols[:, :], colsd[:, :], writes=(b_const,))
        T.dma("sp", decT[:].rearrange("p a b -> p (a b)"), tabs[:, 0:1024], writes=(b_const,))
        T.dma("sp", xi0[:].rearrange("p a b -> p (a b)"), tabs[:, 1024:2048], writes=(b_const,))
        T.dma("pool", cb[:, :], cbf[:, :], writes=(b_const,))

        def col(c):
            return cols[:, c:c + 1]

        def rmsnorm(gcol0, out_fn, out_bufs):
            bk = bank()
            for g4 in range(4):
                sqv = SQ[:, g4 % 2]
                act(sqv, xT[:, g4 * 4:(g4 + 1) * 4, :], AF.Square, reads=b_x[g4 * 4:(g4 + 1) * 4],
                    writes=(b_sq[g4 % 2],))
                pairs = [(ones, sqv[:, s, :]) for s in range(4)]

                def fn(pe, pairs=pairs, g4=g4, bk=bk):
                    ins = None
                    for i, (l, r) in enumerate(pairs):
                        ins = pe.matmul(psA[:, bk, :], l, r, start=(g4 == 0 and i == 0), stop=(g4 == 3 and i == 3))
                    return ins
                T.emit("pe", fn, reads=(b_sq[g4 % 2], b_const), writes=(b_psA[bk],))
            rsqrt_(rstd[:, :], psA[:, bk, :], 1.0 / D, (b_psA[bk],), (b_rstd,))
            for s in range(16):
                stt(out_fn(s), xT[:, s, :], col(gcol0 + s), rstd[:, :], ALU.mult, ALU.mult,
                    reads=(b_x[s], b_rstd, b_const), writes=(out_bufs[s],))

        def tm_block(slab0, evac):
            slots = [ring_next(("in", slab0 + i)) for i in range(4)]
            s0 = slots[0]
            assert s0 % 4 == 0 and slots == [s0, s0 + 1, s0 + 2, s0 + 3]
            for n in range(NCH):
                bk = bank()
                pairs = [(hT[:, k, n * 128:(n + 1) * 128], ring[:, s0:s0 + 4, k * 128:(k + 1) * 128]) for k in range(16)]
                mm([(psA[:, bk, :], pairs)], reads=b_h + [b_slot[s] for s in slots], writes=(b_psA[bk],))
                evac(n, bk)
            for s_ in slots:
                ring_release(s_)

        def rotary_evac(dst_fn, dst_bufs, cbk):
            t1 = scr(0, [128, 2, 512], F32)
            t2 = scr(4, [128, 2, 512], F32)
            cnt = [0]

            def ev(n, bk):
                i = cnt[0] % 2
                cnt[0] += 1
                ps4 = psA[:, bk, :].rearrange("p (h t i) -> p h t i", h=4, t=2)
                t1v = t1[:, i, :].rearrange("p (h t i) -> p h t i", h=4, t=2)
                t2v = t2[:, i, :].rearrange("p (h t i) -> p h t i", h=4, t=2)
                cosb = rot[:, 0, n, :].unsqueeze(1).unsqueeze(1).to_broadcast([128, 4, 2, 64])
                nsinb = rot[:, 1, n, :].unsqueeze(1).to_broadcast([128, 4, 64])
                sinb = rot[:, 2, n, :].unsqueeze(1).to_broadcast([128, 4, 64])
                tt_(t1v, ps4, cosb, ALU.mult, reads=(b_psA[bk], b_rot), writes=sbufs(0 + 2 * i, 2))
                tt_(t2v[:, :, 0, :], ps4[:, :, 1, :], nsinb, ALU.mult, reads=(b_psA[bk], b_rot), writes=sbufs(4 + 2 * i, 2))
                tt_(t2v[:, :, 1, :], ps4[:, :, 0, :], sinb, ALU.mult, reads=(b_psA[bk], b_rot), writes=sbufs(4 + 2 * i, 2))
                tt_(dst_fn(n, cbk[0]), t1[:, i, :], t2[:, i, :], ALU.add, reads=sbufs(0 + 2 * i, 2) + sbufs(4 + 2 * i, 2),
                    writes=(dst_bufs[n],))
            return ev

        def rg_slab(o):
            s_ = ring_next(("in", SL_RG + o))
            pb = o % 2
            mm([(psB[:, pb * 512:(pb + 1) * 512], [(ring[:, s_, k * 128:(k + 1) * 128], hT[:, k, :]) for k in range(16)])],
               reads=b_h + [b_slot[s_]], writes=(b_psBk[pb],))
            ring_release(s_)
            n_, w_, half = o // 4, (o % 4) // 2, o % 2
            dbuf = (b_q if w_ == 0 else b_k)[n_]
            act(QK[:, w_, n_, half * 512:(half + 1) * 512], psB[:, pb * 512:(pb + 1) * 512], AF.Silu,
                reads=(b_psBk[pb],), writes=(dbuf,))

        for tt in tiles:
            T.dma("sp", xT[:], xin[tt], writes=b_x)
            T.dma("sp", rot[:], rott[tt], writes=(b_rot,))
            for lidx, li in enumerate(layers):
              try:
                    cbase = li * C_PER_LAYER
                    T.dma("sp", gbc[:, :], sgug[li].partition_broadcast(128), writes=(b_layer,))
                    T.dma("pool", sgb[:, :], sgub[li], writes=(b_layer,))
                    T.dma("pool", WmT[:].rearrange("p a b -> p (a b)"), sguw[li], writes=(b_layer,))
                    tt_(WmT[:], WmT[:], maskT.unsqueeze(1).to_broadcast([128, 8, 128]), ALU.mult,
                        reads=(b_layer, b_const), writes=(b_layer,))
                    T.dma("pool", pT[:], pin[li, tt], writes=(b_pT,))

                    rmsnorm(cbase + C_MIX, lambda s: hT[:, s, :], b_h)

                    if stop == 'A':
                        raise _Stop()
                    cbk = [0]
                    for blk in range(2):
                        cbk[0] = blk
                        tm_block(SL_K + 4 * blk,
                                 rotary_evac(lambda n, c: QK[:, 1, n, c * 512:(c + 1) * 512], b_k, cbk))
                    for blk in range(4):
                        def ev_rv(n, bk, blk=blk):
                            act(RV[:, n, blk * 512:(blk + 1) * 512], psA[:, bk, :], AF.Copy, reads=(b_psA[bk],), writes=(b_rv[n],))
                        tm_block(SL_RV + 4 * blk, ev_rv)
                    for blk in range(2):
                        cbk[0] = blk
                        tm_block(SL_Q + 4 * blk,
                                 rotary_evac(lambda n, c: QK[:, 0, n, c * 512:(c + 1) * 512], b_q, cbk))

                    if stop == 'B':
                        raise _Stop()
                    kz = scr(8, [128, 2, 1024], BF16)
                    for n in range(NCH):
                        i = n % 2
                        zb = cols[:, C_ZETA:C_ZETA + 8].unsqueeze(2).to_broadcast([128, 8, 128])
                        tt_(kz[:, i, :].rearrange("p (h d) -> p h d", h=8), QK[:, 1, n, :].rearrange("p (h d) -> p h d", h=8),
                            zb, ALU.mult, reads=(b_k[n], b_const), writes=sbufs(8 + 2 * i, 2))
                        bk = bank(4)
                        groups = []
                        for h in range(H):
                            o = psA[:, bk + h // 2, (h % 2) * 256:(h % 2) * 256 + 256]
                            groups.append((o, [(kz[:, i, h * 128:(h + 1) * 128], RV[:, n, h * 256:(h + 1) * 256])]))
                        mm(groups, reads=sbufs(8 + 2 * i, 2) + [b_rv[n]], writes=b_psA[bk:bk + 4])
                        for h in range(H):
                            o = psA[:, bk + h // 2, (h % 2) * 256:(h % 2) * 256 + 256]
                            if n == 0:
                                T.emit("dve", lambda e, o=o, h=h: e.tensor_copy(out=ST[:, h * 256:(h + 1) * 256], in_=o),
                                       reads=(b_psA[bk + h // 2],), writes=(b_st,))
                            else:
                                stt(ST[:, h * 256:(h + 1) * 256], ST[:, h * 256:(h + 1) * 256], col(C_CDN + 8 + h), o,
                                    ALU.mult, ALU.add, reads=(b_psA[bk + h // 2], b_st, b_const), writes=(b_st,))
                        if n < 3:
                            act(PN[:, n, :], ST[:, :], AF.Copy, reads=(b_st,), writes=(b_pn[n],))
                    T.dma("sp", agL_in[tt][li][:, :], ST[:, :], reads=(b_st,), writes=(b_agL[tt][li],))
                    T.wait_all("pool", (b_agL[tt][li],))
                    ccs = T.new_sem("cc")
                    T.engs["pool"].ops.append(
                        lambda eng, a=agL_in[tt][li], b=agL_out[tt][li], ccs=ccs: eng.collective_compute(
                            "AllGather", ALU.bypass, replica_groups=[[0, 1, 2, 3], [4, 5, 6, 7]],
                            ins=[a.ap().opt()], outs=[b.ap().opt()]).then_inc(ccs, 1))
                    T._mark((id(ccs), 1), (), (b_agL[tt][li],))

                    if stop == 'C':
                        raise _Stop()
                    gst = uT[:].rearrange("p a b -> p (a b)").bitcast(F32)
                    gv2 = scr(2, [128, 512], F32)
                    vn = scr(4, [128, 2, 1024], BF16)
                    sS = scr(8, [128, NCH, 1024], BF16)
                    junk = scr(16, [128, 512], F32)
                    for blk in range(2):
                        slots = [ring_next(("in", SL_V + 4 * blk + i)) for i in range(4)]
                        s0 = slots[0]
                        assert s0 % 4 == 0
                        for n in range(NCH):
                            bk = bank()
                            pairs = [(hT[:, k, n * 128:(n + 1) * 128], ring[:, s0:s0 + 4, k * 128:(k + 1) * 128]) for k in range(16)]
                            mm([(psA[:, bk, :], pairs)], reads=b_h + [b_slot[s_] for s_ in slots], writes=(b_psA[bk],))
                            if blk == 0:
                                act(gst[:, n * 512:(n + 1) * 512], psA[:, bk, :], AF.Gelu_apprx_tanh,
                                    reads=(b_psA[bk],), writes=(b_u[2 * n], b_u[2 * n + 1]))
                                act(junk, gst[:, n * 512:(n + 1) * 512], AF.Square,
                                    reads=(b_u[2 * n], b_u[2 * n + 1]), writes=sbufs(16, 2))
                                T.emit("dve", lambda e, n=n: e.tensor_reduce(out=ssq[:, n:n + 1], in_=junk, axis=mybir.AxisListType.X, op=ALU.add),
                                       reads=sbufs(16, 2), writes=(b_ssq,))
                            else:
                                i = n % 2
                                act(gv2, psA[:, bk, :], AF.Gelu_apprx_tanh, reads=(b_psA[bk],), writes=sbufs(2, 2))
                                act(junk, gv2, AF.Square, reads=sbufs(2, 2), writes=sbufs(16, 2))
                                T.emit("dve", lambda e, n=n: e.tensor_reduce(out=ssq[:, 4 + n:5 + n], in_=junk, axis=mybir.AxisListType.X, op=ALU.add),
                                       reads=sbufs(16, 2), writes=(b_ssq,))
                                tt_(ssq[:, n:n + 1], ssq[:, n:n + 1], ssq[:, 4 + n:5 + n], ALU.add, reads=(b_ssq,), writes=(b_ssq,))
                                rsqrt_(ssq[:, n:n + 1], ssq[:, n:n + 1], 1.0 / 1024, (b_ssq,), (b_ssq,))
                                stt(vn[:, i, 0:512], gst[:, n * 512:(n + 1) * 512], ssq[:, n:n + 1], gbc[:, 0:512], ALU.mult, ALU.mult,
                                    reads=(b_u[2 * n], b_u[2 * n + 1], b_ssq, b_layer), writes=sbufs(4 + 2 * i, 2))
                                stt(vn[:, i, 512:1024], gv2, ssq[:, n:n + 1], gbc[:, 512:1024], ALU.mult, ALU.mult,
                                    reads=sbufs(2, 2) + [b_ssq, b_layer], writes=sbufs(4 + 2 * i, 2))
                                bk2 = bank(2)
                                groups = []
                                for g in range(8):
                                    o = psA[:, bk2 + g // 4, (g % 4) * 128:(g % 4) * 128 + 128]
                                    groups.append((o, [(vn[:, i, g * 128:(g + 1) * 128], WmT[:, g, :]),
                                                       (cb[0:1, 0:128], sgb[0:1, g * 128:(g + 1) * 128])]))
                                mm(groups, reads=sbufs(4 + 2 * i, 2) + [b_layer, b_const], writes=b_psA[bk2:bk2 + 2])
                                act(sS[:, n, :], psA[:, bk2:bk2 + 2, :].rearrange("p a b -> p (a b)"), AF.Copy,
                                    reads=b_psA[bk2:bk2 + 2], writes=sbufs(8 + 2 * n, 2))
                        for s_ in slots:
                            ring_release(s_)
                    for g in range(8):
                        s_ = ring_next(("in", SL_U + g))
                        bk = bank()
                        mm([(psA[:, bk, :], [(ring[:, s_, k * 128:(k + 1) * 128], hT[:, k, :]) for k in range(16)])],
                           reads=b_h + [b_slot[s_]], writes=(b_psA[bk],))
                        ring_release(s_)
                        ug = scr(16 + 2 * (g % 2), [128, 512], F32)
                        act(ug, psA[:, bk, :], AF.Gelu_apprx_tanh, reads=(b_psA[bk],), writes=sbufs(16 + 2 * (g % 2), 2))
                        tt_(uT[:, g, :].rearrange("p (n t) -> p n t", n=NCH), ug.rearrange("p (n t) -> p n t", n=NCH),
                            sS[:, :, g * 128:(g + 1) * 128], ALU.mult,
                            reads=sbufs(16 + 2 * (g % 2), 2) + sbufs(8, 8), writes=(b_u[g],))

                    if stop == 'D':
                        raise _Stop()
                    nterm = 3 if tt == 0 else 7
                    T.wait_all("sp", (b_agL[tt][li],))
                    first = True
                    for term in range(nterm):
                        if tt == 0:
                            src = agL_out[0][li][term * 128:(term + 1) * 128, :]
                            sbuf_src = b_agL[0][li]
                        elif term < 4:
                            src = agL_out[0][li][term * 128:(term + 1) * 128, :]
                            sbuf_src = b_agL[0][li]
                        else:
                            src = agL_out[1][li][(term - 4) * 128:(term - 3) * 128, :]
                            sbuf_src = b_agL[1][li]
                        i = term % 2
                        stg = scr(8 * i, [128, 2048], F32)
                        T.dma("sp", stg, src, reads=(sbuf_src,), writes=sbufs(8 * i, 8))
                        for h in range(H):
                            cc = col(C_COEF + (tt * 7 + term) * 8 + h)
                            if first:
                                ts_(ST[:, h * 256:(h + 1) * 256], stg[:, h * 256:(h + 1) * 256], cc, None, ALU.mult, None,
                                    reads=sbufs(8 * i, 8) + [b_const], writes=(b_st,))
                            else:
                                stt(ST[:, h * 256:(h + 1) * 256], stg[:, h * 256:(h + 1) * 256], cc, ST[:, h * 256:(h + 1) * 256],
                                    ALU.mult, ALU.add, reads=sbufs(8 * i, 8) + [b_const, b_st], writes=(b_st,))
                        first = False

                    if stop == 'E':
                        raise _Stop()
                    kTs = scr(0, [128, 8, 128], BF16)
                    qTs = scr(2, [128, 8, 128], BF16)
                    qxT = scr(4, [128, 8, 128], BF16)
                    smS = scr(6, [128, 8, 128], BF16)
                    prv = scr(8, [128, 2048], BF16)
                    ysq = scr(12, [128, 16, 128], BF16)
                    ysm = scr(16, [128, 8, 128], BF16)
                    rsh = scr(18, [128, 8, 128], F32)
                    for n in range(NCH):
                        def fn(pe, n=n):
                            ins = None
                            for w in range(2):
                                for h in range(H):
                                    j = w * 8 + h
                                    ins = pe.matmul(psA[:, j // 4, (j % 4) * 128:(j % 4) * 128 + 128],
                                                    QK[:, w, n, h * 128:(h + 1) * 128], ident, start=True, stop=True)
                            return ins
                        T.emit("pe", fn, reads=(b_q[n], b_k[n], b_const), writes=b_psA[0:4])
                        if stop == 'F0a':
                            raise _Stop()
                        psQ = psA[:, 0:2, :].rearrange("p a b -> p (a b)")
                        psK = psA[:, 2:4, :].rearrange("p a b -> p (a b)")
                        act(qTs[:].rearrange("p a b -> p (a b)"), psQ, AF.Copy, reads=b_psA[0:2], writes=sbufs(2, 2))
                        act(kTs[:].rearrange("p a b -> p (a b)"), psK, AF.Copy, reads=b_psA[2:4], writes=sbufs(0, 2))
                        if stop == 'F0b':
                            raise _Stop()
                        for bb in range(2):
                            tt_(qxT[:].rearrange("p a b -> p (a b)")[:, bb * 512:(bb + 1) * 512],
                                qTs[:].rearrange("p a b -> p (a b)")[:, bb * 512:(bb + 1) * 512],
                                xi0[:].rearrange("p a b -> p (a b)")[:, bb * 512:(bb + 1) * 512], ALU.mult,
                                reads=sbufs(2, 2) + [b_const], writes=sbufs(4, 2))
                        if stop == 'F0c':
                            raise _Stop()
                        for h in range(H):
                            if n == 0:
                                T.emit("dve", lambda e, h=h: e.tensor_copy(out=prv[:, h * 256:(h + 1) * 256], in_=ST[:, h * 256:(h + 1) * 256]),
                                       reads=(b_st,), writes=sbufs(8, 4))
                            else:
                                stt(prv[:, h * 256:(h + 1) * 256], ST[:, h * 256:(h + 1) * 256], col(C_CDN + 8 * n + h),
                                    PN[:, n - 1, h * 256:(h + 1) * 256], ALU.mult, ALU.add,
                                    reads=(b_st, b_pn[n - 1], b_const), writes=sbufs(8, 4))
                        if stop == 'F1':
                            raise _Stop()
                        bs = 4
                        groups = []
                        for h in range(H):
                            o = psA[:, bs + h // 4, (h % 4) * 128:(h % 4) * 128 + 128]
                            groups.append((o, [(kTs[:, h, :], qTs[:, h, :])]))
                        mm(groups, reads=sbufs(0, 4), writes=b_psA[bs:bs + 2])
                        for bb in range(2):
                            tt_(smS[:].rearrange("p a b -> p (a b)")[:, bb * 512:(bb + 1) * 512], psA[:, bs + bb, :],
                                decT[:].rearrange("p a b -> p (a b)")[:, bb * 512:(bb + 1) * 512], ALU.mult,
                                reads=[b_psA[bs + bb], b_const], writes=sbufs(6, 2))
                        if stop == 'F2':
                            raise _Stop()
                        by = 0
                        pa[0] = 0
                        groups = []
                        for h in range(H):
                            for e2 in range(2):
                                sl = 2 * h + e2
                                o = psA[:, by + sl // 4, (sl % 4) * 128:(sl % 4) * 128 + 128]
                                groups.append((o, [(RV[:, n, h * 256 + e2 * 128:h * 256 + e2 * 128 + 128], smS[:, h, :]),
                                                   (prv[:, h * 256 + e2 * 128:h * 256 + e2 * 128 + 128], qxT[:, h, :])]))
                        mm(groups, reads=[b_rv[n]] + sbufs(4, 8), writes=b_psA[by:by + 4])
                        rg_slab(4 * n)
                        rg_slab(4 * n + 1)
                        if stop == 'F3':
                            raise _Stop()
                        psY = psA[:, by:by + 4, :].rearrange("p a b -> p (a b)")
                        for bb in range(2):
                            act(ysq[:].rearrange("p a b -> p (a b)")[:, bb * 1024:(bb + 1) * 1024],
                                psA[:, by + 2 * bb:by + 2 * bb + 2, :].rearrange("p a b -> p (a b)"), AF.Square,
                                reads=b_psA[by + 2 * bb:by + 2 * bb + 2], writes=sbufs(12, 4))
                        ysv = ysq[:].rearrange("p (h e) t -> p h e t", e=2)
                        tt_(ysm[:], ysv[:, :, 0, :], ysv[:, :, 1, :], ALU.add, reads=sbufs(12, 4), writes=sbufs(16, 2))
                        bn = bs
                        ysf = ysm[:].rearrange("p a b -> p (a b)")
                        mm([(psA[:, bn, :], [(ones, ysf[:, 0:512])]), (psA[:, bn + 1, :], [(ones, ysf[:, 512:1024])])],
                           reads=sbufs(16, 2) + [b_const], writes=b_psA[bn:bn + 2])
                        rg_slab(4 * n + 2)
                        rg_slab(4 * n + 3)
                        rsf = rsh[:].rearrange("p a b -> p (a b)")
                        rsqrt_(rsf, psA[:, bn:bn + 2, :].rearrange("p a b -> p (a b)"), 1.0 / 256, b_psA[bn:bn + 2], sbufs(18, 4))
                        if stop == 'F4':
                            raise _Stop()
                        for bb in range(4):
                            tt_(RV[:, n, bb * 512:(bb + 1) * 512].rearrange("p (h e t) -> p h e t", h=2, e=2),
                                psA[:, by + bb, :].rearrange("p (h e t) -> p h e t", h=2, e=2),
                                rsh[:, 2 * bb:2 * bb + 2, :].unsqueeze(2).to_broadcast([128, 2, 2, 128]), ALU.mult,
                                reads=[b_psA[by + bb]] + sbufs(18, 4), writes=(b_rv[n],))

                    if stop == 'F':
                        raise _Stop()
                    for o in range(16):
                        n_, w_, half = o // 4, (o % 4) // 2, o % 2
                        dbuf = (b_q if w_ == 0 else b_k)[n_]
                        tgv = QK[:, w_, n_, half * 512:(half + 1) * 512].rearrange("p (n t) -> p n t", n=NCH)
                        yv = RV[:, :, o * 128:(o + 1) * 128]
                        stt(yv, tgv, col(cbase + C_RET + o), yv, ALU.mult, ALU.mult,
                            reads=[dbuf, b_const] + b_rv, writes=b_rv)

                    mT = QK[:].rearrange("p a b c -> p (a b c)").rearrange("p (s t) -> p s t", t=TT)
                    b_m = b_q + b_k
                    wa_slot = None
                    for o in range(16):
                        s = ring_next(("in", SL_GA + o))
                        bk = bank()
                        mm([(psA[:, bk, :], [(ring[:, s, k * 128:(k + 1) * 128], hT[:, k, :]) for k in range(16)])],
                           reads=b_h + [b_slot[s]], writes=(b_psA[bk],))
                        ring_release(s)
                        i = o % 2
                        sg = scr(4 + 2 * i, [128, 512], F32)
                        act(sg, psA[:, bk, :], AF.Sigmoid, reads=(b_psA[bk],), writes=sbufs(4 + 2 * i, 2))
                        if o % 2 == 0:
                            wa_slot = ring_next(("wa", o // 2))
                        bk = bank()
                        wa_v = ring[:, wa_slot, :].rearrange("p (o k m) -> p o k m", o=2, k=8)
                        mm([(psA[:, bk, :], [(wa_v[:, o % 2, k, :], uT[:, k, :]) for k in range(8)])],
                           reads=b_u + [b_slot[wa_slot]], writes=(b_psA[bk],))
                        if o % 2 == 1:
                            ring_release(wa_slot)
                        tt_(sg, sg, psA[:, bk, :], ALU.mult, reads=sbufs(4 + 2 * i, 2) + [b_psA[bk]], writes=sbufs(4 + 2 * i, 2))
                        s = ring_next(("in", SL_GB + o))
                        bk = bank()
                        mm([(psA[:, bk, :], [(ring[:, s, k * 128:(k + 1) * 128], hT[:, k, :]) for k in range(16)])],
                           reads=b_h + [b_slot[s]], writes=(b_psA[bk],))
                        ring_release(s)
                        sg2 = scr(8 + 2 * i, [128, 512], F32)
                        act(sg2, psA[:, bk, :], AF.Sigmoid, reads=(b_psA[bk],), writes=sbufs(8 + 2 * i, 2))
                        s = ring_next(("wb", o))
                        bk = bank()
                        mm([(psA[:, bk, :], [(ring[:, s, k * 128:(k + 1) * 128], RV[:, :, k * 128:(k + 1) * 128]) for k in range(16)])],
                           reads=b_rv + [b_slot[s]], writes=(b_psA[bk],))
                        ring_release(s)
                        tt_(sg2, sg2, psA[:, bk, :], ALU.mult, reads=sbufs(8 + 2 * i, 2) + [b_psA[bk]], writes=sbufs(8 + 2 * i, 2))
                        tt_(mT[:, o, :], sg, sg2, ALU.add, reads=sbufs(4 + 2 * i, 2) + sbufs(8 + 2 * i, 2),
                            writes=b_m[4 * (o // 8):4 * (o // 8) + 4])

                    for o in range(16):
                        s = ring_next(("wout", o))
                        bk = bank()
                        mm([(psA[:, bk, :], [(ring[:, s, k * 128:(k + 1) * 128], mT[:, k, :]) for k in range(16)])],
                           reads=b_m + [b_slot[s]], writes=(b_psA[bk],))
                        ring_release(s)
                        tt_(xT[:, o, :], xT[:, o, :], psA[:, bk, :], ALU.add, reads=(b_x[o], b_psA[bk]), writes=(b_x[o],))

                    rmsnorm(cbase + C_FFN, lambda s: hT[:, s, :], b_h)
                    T.emit("dve", lambda e: e.tensor_copy(out=hsend[:], in_=hT[:, :, TT - 2:TT]), reads=b_h, writes=(b_hsend,))
                    T.dma("sp", agH_in[tt][li][:, :], hsend[:].rearrange("p a b -> p (a b)"), reads=(b_hsend,), writes=(b_agH[tt][li],))
                    T.wait_all("pool", (b_agH[tt][li],))
                    cch = T.new_sem("ch")
                    T.engs["pool"].ops.append(
                        lambda eng, a=agH_in[tt][li], b=agH_out[tt][li], cch=cch: eng.collective_compute(
                            "AllGather", ALU.bypass, replica_groups=[[0, 1, 2, 3], [4, 5, 6, 7]],
                            ins=[a.ap().opt()], outs=[b.ap().opt()]).then_inc(cch, 1))
                    T._mark((id(cch), 1), (), (b_agH[tt][li],))
                    T.wait_all("sp", (b_agH[tt][li],))
                    T.dma("sp", hal[:, 0], agH_out[tt][li][:, :].rearrange("(r p) f -> p r f", p=128),
                          reads=(b_agH[tt][li],), writes=(b_hal,))
                    T.dma("sp", hal[:, 1], agH_out[0][li][:, :].rearrange("(r p) f -> p r f", p=128),
                          reads=(b_agH[0][li],), writes=(b_hal,))
                    for t8 in range(8):
                        src = hal[:, t8 // 4, t8 % 4, :]
                        cc = col(C_SEL + tt * 8 + t8)
                        if t8 == 0:
                            ts_(hacc[:, :], src, cc, None, ALU.mult, None, reads=(b_hal, b_const), writes=(b_hacc,))
                        else:
                            stt(hacc[:, :], src, cc, hacc[:, :], ALU.mult, ALU.add, reads=(b_hal, b_const, b_hacc), writes=(b_hacc,))
                    T.emit("dve", lambda e: e.tensor_copy(out=hh[:].rearrange("p a b -> p (a b)"), in_=hacc[:, :]),
                           reads=(b_hacc,), writes=(b_hh,))

                    aS = scr(0, [128, 2, 512], F32)
                    geS = scr(4, [128, 2, 512], F32)
                    actT = scr(8, [128, 2, FG, 512], BF16)
                    for fg in range(NFG):
                        ia = fg % 2
                        for f in range(FG):
                            ff = fg * FG + f
                            i = ff % 2
                            s_ = ring_next(("gate", ff))
                            bk = bank()
                            mm([(psA[:, bk, :], [(ring[:, s_, k * 128:(k + 1) * 128], hT[:, k, :]) for k in range(16)])],
                               reads=b_h + [b_slot[s_]], writes=(b_psA[bk],))
                            hc = ff % 8
                            hp = psB[:, hc * 2:hc * 2 + 2]
                            mm([(hp, [(ring[:, s_, k * 128:(k + 1) * 128], hh[:, k, :]) for k in range(16)])],
                               reads=[b_hh, b_slot[s_]], writes=(b_psB,))
                            ring_release(s_)
                            cw0, cw1, cw2 = (col(cbase + C_CW + t * NF + ff) for t in range(3))
                            cbb = col(cbase + C_CB + ff)
                            a_ = aS[:, i, :]
                            ab = sbufs(2 * i, 2)
                            act(a_, psA[:, bk, :], AF.Identity, reads=(b_psA[bk], b_const), writes=ab, bias=cbb, scale=cw2)
                            stt(a_[:, 1:512], psA[:, bk, 0:511], cw1, a_[:, 1:512], ALU.mult, ALU.add,
                                reads=[b_psA[bk], b_const] + ab, writes=ab)
                            stt(a_[:, 2:512], psA[:, bk, 0:510], cw0, a_[:, 2:512], ALU.mult, ALU.add,
                                reads=[b_psA[bk], b_const] + ab, writes=ab)
                            stt(a_[:, 0:2], hp, cw0, a_[:, 0:2], ALU.mult, ALU.add,
                                reads=[b_psB, b_const] + ab, writes=ab)
                            stt(a_[:, 0:1], hp[:, 1:2], cw1, a_[:, 0:1], ALU.mult, ALU.add,
                                reads=[b_psB, b_const] + ab, writes=ab)
                            act(geS[:, i, :], a_, AF.Gelu_apprx_tanh, reads=ab, writes=sbufs(4 + 2 * i, 2))
                            s_ = ring_next(("up", ff))
                            bk = bank()
                            mm([(psA[:, bk, :], [(ring[:, s_, k * 128:(k + 1) * 128], hT[:, k, :]) for k in range(16)])],
                               reads=b_h + [b_slot[s_]], writes=(b_psA[bk],))
                            ring_release(s_)
                            tt_(actT[:, ia, f, :], geS[:, i, :], psA[:, bk, :], ALU.mult,
                                reads=sbufs(4 + 2 * i, 2) + [b_psA[bk]], writes=sbufs(8 + 4 * ia, 4))
                        dslots = [ring_next(("down", fg, q)) for q in range(4)]
                        for o in range(16):
                            s_ = dslots[o // 4]
                            dv = ring[:, s_, :].rearrange("p (k c) -> p k c", k=FG)
                            bk = bank()
                            mm([(psA[:, bk, :], [(dv[:, kc, (o % 4) * 128:(o % 4) * 128 + 128], actT[:, ia, kc, :]) for kc in range(FG)])],
                               reads=sbufs(8 + 4 * ia, 4) + [b_slot[s_]], writes=(b_psA[bk],))
                            if o % 4 == 3:
                                ring_release(s_)
                            tt_(xT[:, o, :], xT[:, o, :], psA[:, bk, :], ALU.add, reads=(b_x[o], b_psA[bk]), writes=(b_x[o],))

                    rmsnorm(cbase + C_PLE, lambda s: hT[:, s, :], b_h)
                    pp_slot = None
                    for o in range(16):
                        s = ring_next(("pg", o))
                        bk = bank()
                        mm([(psA[:, bk, :], [(ring[:, s, k * 128:(k + 1) * 128], hT[:, k, :]) for k in range(16)])],
                           reads=b_h + [b_slot[s]], writes=(b_psA[bk],))
                        ring_release(s)
                        i = o % 2
                        sg = scr(4 + 2 * i, [128, 512], F32)
                        act(sg, psA[:, bk, :], AF.Sigmoid, reads=(b_psA[bk],), writes=sbufs(4 + 2 * i, 2))
                        if o % 8 == 0:
                            pp_slot = ring_next(("pp", o // 8))
                        pv = ring[:, pp_slot, :].rearrange("p (o k m) -> p o k m", o=8, k=2)
                        bk = bank()
                        mm([(psA[:, bk, :], [(pv[:, o % 8, k, :], pT[:, k, :]) for k in range(2)])],
                           reads=[b_pT, b_slot[pp_slot]], writes=(b_psA[bk],))
                        if o % 8 == 7:
                            ring_release(pp_slot)
                        tt_(sg, sg, psA[:, bk, :], ALU.mult, reads=sbufs(4 + 2 * i, 2) + [b_psA[bk]], writes=sbufs(4 + 2 * i, 2))
                        tt_(xT[:, o, :], xT[:, o, :], sg, ALU.add, reads=[b_x[o]] + sbufs(4 + 2 * i, 2), writes=(b_x[o],))
                    ring_next(None)
                    ring_next(None)

              except _Stop:
                  pass
            if do_final:
                bk = bank()
                for g4 in range(4):
                    sqv = SQ[:, g4 % 2]
                    act(sqv, xT[:, g4 * 4:(g4 + 1) * 4, :], AF.Square, reads=b_x[g4 * 4:(g4 + 1) * 4], writes=(b_sq[g4 % 2],))
                    pairs = [(ones, sqv[:, s_, :]) for s_ in range(4)]

                    def fn(pe, pairs=pairs, g4=g4, bk=bk):
                        ins = None
                        for i, (l, r) in enumerate(pairs):
                            ins = pe.matmul(psA[:, bk, :], l, r, start=(g4 == 0 and i == 0), stop=(g4 == 3 and i == 3))
                        return ins
                    T.emit("pe", fn, reads=(b_sq[g4 % 2], b_const), writes=(b_psA[bk],))
                rsqrt_(rstd[:, :], psA[:, bk, :], 1.0 / D, (b_psA[bk],), (b_rstd,))
                for s_ in range(16):
                    stt(xT[:, s_, :], xT[:, s_, :], col(C_FINAL + s_), rstd[:, :], ALU.mult, ALU.mult,
                        reads=(b_x[s_], b_rstd, b_const), writes=(b_x[s_],))
            T.dma("sp", yout[tt], xT[:], reads=b_x, writes=(b_out,))
        T.wait_all("sp", (b_out,))
        for t_ in range(2):
            T.wait_all("sp", b_agL[t_] + b_agH[t_])

        with nc.Block() as block:
            @block.tensor
            def _(e):
                for f in T.engs["pe"].ops:
                    f(e)

            @block.scalar
            def _(e):
                for f in T.engs["act"].ops:
                    f(e)

            @block.vector
            def _(e):
                for f in T.engs["dve"].ops:
                    f(e)

            @block.gpsimd
            def _(e):
                for f in T.engs["pool"].ops:
                    f(e)

            @block.sync
            def _(e):
                for f in T.engs["sp"].ops:
                    f(e)
    return nc


def pack_weights(w_in, w_a, w_b, w_out, w_gate, w_up, w_down, pg, pp, layers):
    out = np.empty((len(layers), NREAL, 128, 2048), np.float32)

    def fm(W, o):
        return W[:, o * 128:(o + 1) * 128].reshape(16, 128, 128).transpose(1, 0, 2).reshape(128, 2048)

    for lpos, li in enumerate(layers):
        ri = 0
        for d_ in PLAN:
            if d_ is None:
                continue
            kind = d_[0]
            if kind == "in":
                blk = fm(w_in[li], d_[1])
            elif kind == "wa":
                so = d_[1]
                blk = w_a[li][:, so * 256:(so + 1) * 256].reshape(8, 128, 2, 128).transpose(1, 2, 0, 3).reshape(128, 2048)
            elif kind == "wb":
                blk = fm(w_b[li], d_[1])
            elif kind == "wout":
                blk = fm(w_out[li], d_[1])
            elif kind == "gate":
                blk = fm(w_gate[li], d_[1])
            elif kind == "up":
                blk = fm(w_up[li], d_[1])
            elif kind == "down":
                fg, q = d_[1], d_[2]
                blk = w_down[li][fg * 512:(fg + 1) * 512, q * 512:(q + 1) * 512].reshape(4, 128, 512).transpose(1, 0, 2).reshape(128, 2048)
            elif kind == "pg":
                blk = fm(pg[li], d_[1])
            elif kind == "pp":
                sp = d_[1]
                blk = pp[li][:, sp * 1024:(sp + 1) * 1024].reshape(2, 128, 8, 128).transpose(1, 2, 0, 3).reshape(128, 2048)
            out[lpos, ri] = blk
            ri += 1
    return out


def const_tables():
    lg = np.log1p(-np.exp2(-5.0 - np.arange(H, dtype=np.float64)))
    idx = np.arange(128, dtype=np.float64)
    diff = idx[:, None] - idx[None, :]
    dec = np.where(diff >= 0, np.exp(lg[:, None, None] * np.where(diff >= 0, diff, 0.0)[None]), 0.0)
    scale = 128.0 ** -0.5
    decT = (dec.transpose(2, 0, 1) * scale)
    xi = np.exp(lg[:, None] * (idx[None, :] + 1.0))
    xi_bc = np.broadcast_to(xi[None], (128, H, 128))
    tabs = np.concatenate([decT.reshape(128, 1024), xi_bc.reshape(128, 1024)], axis=1).astype(np.float32)
    zeta = (np.exp(lg[:, None] * (127.0 - idx[None, :])).T * scale)
    cd = np.exp(lg * 128.0)
    return tabs, zeta, cd


def make_in_maps(x, p, mix_norm_g, w_in, sgu_norm_g, sgu_w, sgu_b, ret_norm_g, w_branch_a, w_branch_b,
                 w_out, ffn_norm_g, ffn_w_gate, ffn_w_up, ffn_conv_w, ffn_conv_b, ffn_w_down,
                 ple_norm_g, ple_w_gate, ple_w_proj, final_norm_g, layers=(0, 1, 2, 3)):
    f32 = np.float32
    x = np.asarray(x, f32)
    p = np.asarray(p, f32)
    A = lambda a: np.asarray(a, f32)
    wst = pack_weights(A(w_in), A(w_branch_a), A(w_branch_b), A(w_out), A(ffn_w_gate), A(ffn_w_up),
                       A(ffn_w_down), A(ple_w_gate), A(ple_w_proj), tuple(layers))
    tabs, zeta, cd = const_tables()

    def colsT(v):
        return np.asarray(v, f32).reshape(-1, 128).T

    cols_base = np.zeros((128, C_TOTAL), f32)
    for li in range(NL):
        b = li * C_PER_LAYER
        cols_base[:, b + C_MIX:b + C_MIX + 16] = colsT(mix_norm_g[li])
        cols_base[:, b + C_FFN:b + C_FFN + 16] = colsT(ffn_norm_g[li])
        cols_base[:, b + C_PLE:b + C_PLE + 16] = colsT(ple_norm_g[li])
        cols_base[:, b + C_RET:b + C_RET + 16] = colsT(ret_norm_g[li])
        cw = np.asarray(ffn_conv_w[li], f32)
        for t in range(3):
            cols_base[:, b + C_CW + t * NF:b + C_CW + (t + 1) * NF] = colsT(cw[t])
        cols_base[:, b + C_CB:b + C_CB + NF] = colsT(ffn_conv_b[li])
    cols_base[:, C_FINAL:C_FINAL + 16] = colsT(final_norm_g)
    cols_base[:, C_EPS] = EPS
    cols_base[:, C_ZETA:C_ZETA + 8] = zeta.astype(f32)
    for n in range(4):
        cols_base[:, C_CDN + 8 * n:C_CDN + 8 * n + 8] = (cd ** n).astype(f32)[None, :]

    sgug = np.asarray(sgu_norm_g, f32).reshape(NL, 1, 1024)
    sgub = np.asarray(sgu_b, f32).reshape(NL, 1, 1024)
    sguw = np.ascontiguousarray(np.asarray(sgu_w, f32).transpose(0, 3, 1, 2)).reshape(NL, 128, 1024)
    cbf = np.zeros((128, 384), f32)
    cbf[:, 0:128] = 1.0
    cbf[:, 128:256] = np.eye(128, dtype=f32)
    ii = np.arange(128)
    cbf[:, 256:384] = (ii[None, :] >= ii[:, None]).astype(f32)

    inv = np.power(f32(10000.0), -(np.arange(64, dtype=f32) / f32(64)))
    in_maps = []
    for c in range(8):
        b, j = c // 4, c % 4
        xin = np.empty((2, 128, 16, TT), f32)
        pin = np.empty((NL, 2, 128, 2, TT), f32)
        rott = np.empty((2, 128, 3, NCH, 64), f32)
        cols_c = cols_base.copy()
        for tt in range(2):
            g = 4 * tt + j
            xs = x[b, g * TT:(g + 1) * TT, :]
            xin[tt] = xs.T.reshape(16, 128, TT).transpose(1, 0, 2)
            for li in range(NL):
                ps_ = p[li, b, g * TT:(g + 1) * TT, :]
                pin[li, tt] = ps_.T.reshape(2, 128, TT).transpose(1, 0, 2)
            pos = (g * TT + np.arange(TT)).astype(f32)
            ang = (pos[:, None] * inv[None, :]).astype(f32)
            cs = np.cos(ang).astype(f32).reshape(NCH, 128, 64).transpose(1, 0, 2)
            sn = np.sin(ang).astype(f32).reshape(NCH, 128, 64).transpose(1, 0, 2)
            rott[tt, :, 0] = cs
            rott[tt, :, 1] = -sn
            rott[tt, :, 2] = sn
            for term in range(7):
                if tt == 0:
                    cf = cd ** (4 * (j - 1 - term)) if term < min(j, 3) and term < 3 else np.zeros(H)
                    if term >= 3:
                        cf = np.zeros(H)
                else:
                    if term < 4:
                        cf = cd ** (4 * (j + 3 - term))
                    else:
                        r = term - 4
                        cf = cd ** (4 * (j - 1 - r)) if r < j else np.zeros(H)
                cols_c[:, C_COEF + (tt * 7 + term) * 8:C_COEF + (tt * 7 + term) * 8 + 8] = cf.astype(f32)[None, :]
            sel = np.zeros(8, f32)
            if j > 0:
                sel[j - 1] = 1.0
            elif tt == 1:
                sel[4 + 3] = 1.0
            cols_c[:, C_SEL + tt * 8:C_SEL + tt * 8 + 8] = sel[None, :]
        in_maps.append({"xin": xin, "pin": pin, "wst": wst, "cols": cols_c, "sgug": sgug, "sgub": sgub,
                        "sguw": sguw, "tabs": tabs, "rott": rott, "cbf": cbf})

    return in_maps


def kernel(**inputs):
    f32 = np.float32
    in_maps = make_in_maps(layers=LAYERS, **inputs)
    nc = build_program(layers=LAYERS)
    res = run_bass_kernel_spmd(nc, in_maps, core_ids=list(range(8)))
    out = np.empty((2, 4096, D), f32)
    for c in range(8):
        b, j = c // 4, c % 4
        y = np.asarray(res.results[c]["yout"])
        for tt in range(2):
            g = 4 * tt + j
            out[b, g * TT:(g + 1) * TT, :] = y[tt].transpose(1, 0, 2).reshape(D, TT).T
    return out
```

```python
import math
from contextlib import ExitStack

import numpy as np
import concourse.bass as bass
import concourse.mybir as mybir
from concourse.bass_utils import run_bass_kernel_spmd

F32 = mybir.dt.float32
BF16 = mybir.dt.bfloat16
AF = mybir.ActivationFunctionType
ALU = mybir.AluOpType

NL = 4
D = 2048
TT = 512
NCH = 4
DFF = 5632
NF = 44
FG = 4
NFG = 11
NSLOT = 8
EPS = 1e-6
H = 8
SL_U, SL_V, SL_Q, SL_K, SL_RV, SL_RG, SL_GA, SL_GB = 0, 8, 16, 24, 32, 48, 64, 80

C_MIX, C_FFN, C_PLE, C_RET, C_CW, C_CB = 0, 16, 32, 48, 64, 196
C_PER_LAYER = 240
C_FINAL = NL * C_PER_LAYER
C_ZETA = C_FINAL + 16
C_CDN = C_ZETA + 8
C_COEF = C_CDN + 32
C_SEL = C_COEF + 112
C_EPS = C_SEL + 16
C_TOTAL = C_EPS + 1


def layer_plan():
    plan = []
    for s in range(8):
        plan.append(("in", SL_K + s))
    for s in range(16):
        plan.append(("in", SL_RV + s))
    for s in range(8):
        plan.append(("in", SL_Q + s))
    for s in range(8):
        plan.append(("in", SL_V + s))
    for s in range(8):
        plan.append(("in", SL_U + s))
    for s in range(16):
        plan.append(("in", SL_RG + s))
    for o in range(16):
        plan.append(("in", SL_GA + o))
        if o % 2 == 0:
            plan.append(("wa", o // 2))
        plan.append(("in", SL_GB + o))
        plan.append(("wb", o))
    for o in range(16):
        plan.append(("wout", o))
    for fg in range(NFG):
        for f in range(FG):
            plan.append(("gate", fg * FG + f))
            plan.append(("up", fg * FG + f))
        for q in range(4):
            plan.append(("down", fg, q))
    for o in range(16):
        plan.append(("pg", o))
        if o % 8 == 0:
            plan.append(("pp", o // 8))
    assert len(plan) == 286
    plan.append(None)
    plan.append(None)
    return plan


PLAN = layer_plan()
NREAL = 286
LAYERS = (0, 1, 2, 3)


class Buf:
    __slots__ = ("w", "r")

    def __init__(self):
        self.w = {}
        self.r = {}


class EngState:
    def __init__(self, name):
        self.name = name
        self.ops = []
        self.sem = None
        self.count = 0
        self.seen = {}


class Tracker:
    ROT = 3000

    def __init__(self, nc, es):
        self.nc = nc
        self.es = es
        self.nsem = 0
        self.sems = {}
        self.engs = {n: EngState(n) for n in ("pe", "act", "dve", "pool", "sp")}
        for e in self.engs.values():
            e.sem = self.new_sem(e.name)
        self.misc = {q: [[self.new_sem("m" + q), 0] for _ in range(4)] for q in ("sp", "pool")}
        self.misc_i = {"sp": 0, "pool": 0}

    def new_sem(self, tag):
        self.nsem += 1
        s = self.es.enter_context(self.nc.semaphore(f"{tag}_{self.nsem}"))
        self.sems[id(s)] = s
        return s

    def _deps(self, reads, writes):
        deps = {}
        for b in reads:
            for k, v in b.w.items():
                if deps.get(k, 0) < v:
                    deps[k] = v
        for b in writes:
            for d in (b.w, b.r):
                for k, v in d.items():
                    if deps.get(k, 0) < v:
                        deps[k] = v
        return deps

    def _waits(self, e, deps):
        for k, v in deps.items():
            if e.seen.get(k, 0) >= v:
                continue
            if k == id(e.sem) and e.name in ("pe", "pool", "sp"):
                continue
            e.seen[k] = v
            sem = self.sems[k]
            e.ops.append(lambda eng, sem=sem, v=v: eng.wait_ge(sem, v))

    def _mark(self, tok, reads, writes):
        k, v = tok
        for b in reads:
            if b.r.get(k, 0) < v:
                b.r[k] = v
        for b in writes:
            if b.r:
                b.w = {k: v}
                b.r = {}
            else:
                b.w[k] = v

    def emit(self, en, fn, reads=(), writes=()):
        e = self.engs[en]
        self._waits(e, self._deps(reads, writes))
        if e.count >= self.ROT:
            e.sem = self.new_sem(e.name)
            e.count = 0
        e.count += 1
        sem, cnt = e.sem, e.count
        e.ops.append(lambda eng, fn=fn, sem=sem: fn(eng).then_inc(sem, 1))
        self._mark((id(sem), cnt), reads, writes)

    def dma(self, en, out, in_, reads=(), writes=(), semslot=None):
        e = self.engs[en]
        self._waits(e, self._deps(reads, writes))
        if semslot is None:
            semslot = self.misc[en][self.misc_i[en]]
            self.misc_i[en] = (self.misc_i[en] + 1) % len(self.misc[en])
        sem, val = semslot
        if val > 0:
            self._waits(e, {id(sem): val})
        if val >= 16 * 180:
            sem = self.new_sem("d")
            semslot[0] = sem
            val = 0
        val += 16
        semslot[1] = val
        e.ops.append(lambda eng, out=out, in_=in_, sem=sem: eng.dma_start(out=out, in_=in_).then_inc(sem, 16))
        self._mark((id(sem), val), reads, writes)

    def wait_all(self, en, bufs):
        e = self.engs[en]
        self._waits(e, self._deps((), bufs))


class _Stop(Exception):
    pass


def build_program(layers=(0, 1, 2, 3), do_final=True, tiles=(0, 1), stop=None):
    nc = bass.Bass("TRN2", target_bir_lowering=False)
    es = ExitStack()
    with es:
        T = Tracker(nc, es)
        xin = nc.dram_tensor("xin", [2, 128, 16, TT], F32, kind="ExternalInput")
        pin = nc.dram_tensor("pin", [NL, 2, 128, 2, TT], F32, kind="ExternalInput")
        wst = nc.dram_tensor("wst", [len(layers), NREAL, 128, 2048], F32, kind="ExternalInput")
        colsd = nc.dram_tensor("cols", [128, C_TOTAL], F32, kind="ExternalInput")
        sgug = nc.dram_tensor("sgug", [NL, 1, 1024], F32, kind="ExternalInput")
        sgub = nc.dram_tensor("sgub", [NL, 1, 1024], F32, kind="ExternalInput")
        sguw = nc.dram_tensor("sguw", [NL, 128, 1024], F32, kind="ExternalInput")
        tabs = nc.dram_tensor("tabs", [128, 2048], F32, kind="ExternalInput")
        rott = nc.dram_tensor("rott", [2, 128, 3, NCH, 64], F32, kind="ExternalInput")
        cbf = nc.dram_tensor("cbf", [128, 384], F32, kind="ExternalInput")
        yout = nc.dram_tensor("yout", [2, 128, 16, TT], F32, kind="ExternalOutput")
        agL_in = [[nc.dram_tensor(f"agLi_{t}_{l}", [128, 2048], F32) for l in range(NL)] for t in range(2)]
        agL_out = [[nc.dram_tensor(f"agLo_{t}_{l}", [512, 2048], F32) for l in range(NL)] for t in range(2)]
        agH_in = [[nc.dram_tensor(f"agHi_{t}_{l}", [128, 32], BF16) for l in range(NL)] for t in range(2)]
        agH_out = [[nc.dram_tensor(f"agHo_{t}_{l}", [512, 32], BF16) for l in range(NL)] for t in range(2)]

        def sb(name, shape, dt):
            return es.enter_context(nc.sbuf_tensor(name, shape, dt))

        xT = sb("xT", [128, 16, TT], F32)
        hT = sb("hT", [128, 16, TT], BF16)
        ring = sb("ring", [128, NSLOT, 2048], BF16)
        QK = sb("QK", [128, 2, NCH, 1024], BF16)
        RV = sb("RV", [128, NCH, 2048], BF16)
        PN = sb("PN", [128, 3, 2048], BF16)
        uT = sb("uT", [128, 8, TT], BF16)
        ST = sb("ST", [128, 2048], F32)
        SCR = sb("SCR", [128, 24 * 1024], mybir.dt.uint8)
        SQ = sb("SQ", [128, 2, 4, TT], BF16)
        rstd = sb("rstd", [128, TT], F32)
        cols = sb("colsS", [128, C_TOTAL], F32)
        decT = sb("decT", [128, 8, 128], F32)
        xi0 = sb("xi0", [128, 8, 128], F32)
        rot = sb("rot", [128, 3, NCH, 64], F32)
        gbc = sb("gbc", [128, 1024], F32)
        WmT = sb("WmT", [128, 8, 128], BF16)
        sgb = sb("sgb", [1, 1024], BF16)
        cb = sb("cbS", [128, 384], BF16)
        pT = sb("pT", [128, 2, TT], BF16)
        hal = sb("hal", [128, 2, 4, 32], BF16)
        hacc = sb("hacc", [128, 32], F32)
        hh = sb("hh", [128, 16, 2], BF16)
        hsend = sb("hsend", [128, 16, 2], BF16)
        ssq = sb("ssq", [128, 8], F32)
        psA = es.enter_context(nc.psum_tensor("psA", [128, 6, 512], F32))
        psB = es.enter_context(nc.psum_tensor("psB", [128, 1024], F32))

        ones = cb[:, 0:128]
        ident = cb[:, 128:256]
        maskT = cb[:, 256:384]

        def scr(off_kib, shape, dt):
            nbytes = int(np.prod(shape[1:])) * (2 if dt == BF16 else 4)
            v = SCR[:, off_kib * 1024: off_kib * 1024 + nbytes].bitcast(dt)
            if len(shape) == 3:
                return v.rearrange("p (a b) -> p a b", b=shape[2])
            if len(shape) == 4:
                return v.rearrange("p (a b c) -> p a b c", b=shape[2], c=shape[3])
            return v

        b_x = [Buf() for _ in range(16)]
        b_h = [Buf() for _ in range(16)]
        b_slot = [Buf() for _ in range(NSLOT)]
        b_q = [Buf() for _ in range(NCH)]
        b_k = [Buf() for _ in range(NCH)]
        b_rv = [Buf() for _ in range(NCH)]
        b_pn = [Buf() for _ in range(3)]
        b_u = [Buf() for _ in range(8)]
        b_st = Buf()
        b_scr = [Buf() for _ in range(24)]
        b_sq = [Buf(), Buf()]
        b_rstd = Buf()
        b_const = Buf()
        b_layer = Buf()
        b_rot = Buf()
        b_pT = Buf()
        b_hal = Buf()
        b_hacc = Buf()
        b_hh = Buf()
        b_hsend = Buf()
        b_hp = [Buf() for _ in range(8)]
        b_ssq = Buf()
        b_psA = [Buf() for _ in range(6)]
        b_psBk = [Buf(), Buf()]
        b_psB = b_psBk[0]
        b_agL = [[Buf() for _ in range(NL)] for _ in range(2)]
        b_agH = [[Buf() for _ in range(NL)] for _ in range(2)]
        b_out = Buf()

        def sbufs(off, n):
            return b_scr[off:off + n]

        pa = [0]

        def bank(n=1):
            p = pa[0]
            if n > 1 and p % n:
                p += n - p % n
            if p + n > 6:
                p = 0
            pa[0] = (p + n) % 6
            return p

        slot_sems = [[T.new_sem("w"), 0] for _ in range(NSLOT)]
        rs = {"issued": 0, "consumed": 0, "seq": [], "released": set(), "slot_req": {}}

        def ring_pump():
            while rs["issued"] < len(rs["seq"]):
                r = rs["issued"]
                if r >= NSLOT and (r - NSLOT) not in rs["released"]:
                    break
                ent = rs["seq"][r]
                s = r % NSLOT
                if ent is not None:
                    li, idx = ent
                    T.dma("pool", ring[:, s, :], wst[li, idx, :, :], reads=(), writes=(b_slot[s],),
                          semslot=slot_sems[s])
                rs["issued"] += 1

        def ring_next(desc):
            r = rs["consumed"]
            assert PLAN[r % len(PLAN)] == desc, (r, PLAN[r % len(PLAN)], desc)
            rs["consumed"] += 1
            if desc is None:
                rs["released"].add(r)
                ring_pump()
                return None
            ring_pump()
            assert rs["issued"] > r, (r, rs["issued"])
            s = r % NSLOT
            rs["slot_req"][s] = r
            return s

        def ring_release(s):
            rs["released"].add(rs["slot_req"][s])
            ring_pump()

        real_idx = {}
        ri = 0
        for i, d_ in enumerate(PLAN):
            if d_ is not None:
                real_idx[i] = ri
                ri += 1
        for tt in tiles:
            for li in layers:
                for i, d_ in enumerate(PLAN):
                    rs["seq"].append(None if d_ is None else (layers.index(li), real_idx[i]))

        def act(out, in_, func, reads, writes, bias=None, scale=None, accum_out=None):
            kw = {}
            if bias is not None:
                kw["bias"] = bias
            if scale is not None:
                kw["scale"] = scale
            if accum_out is not None:
                kw["accum_out"] = accum_out
            T.emit("act", lambda e: e.activation(out=out, in_=in_, func=func, **kw), reads, writes)

        def tt_(out, a, b, op, reads, writes):
            T.emit("dve", lambda e: e.tensor_tensor(out=out, in0=a, in1=b, op=op), reads, writes)

        def ts_(out, a, s1, s2, op0, op1, reads, writes):
            if s2 is None:
                s2, op1 = 0.0, ALU.add
            T.emit("dve", lambda e: e.tensor_scalar(out=out, in0=a, scalar1=s1, scalar2=s2, op0=op0, op1=op1),
                   reads, writes)

        def rsqrt_(out, in_, inv_n, reads_in, wbufs):
            act(out, in_, AF.Sqrt, reads=list(reads_in) + [b_const], writes=wbufs, bias=col(C_EPS), scale=inv_n)
            T.emit("dve", lambda e: e.reciprocal(out=out, in_=out), reads=wbufs, writes=wbufs)

        def stt(out, in0, scalar, in1, op0, op1, reads, writes):
            T.emit("dve", lambda e: e.scalar_tensor_tensor(out=out, in0=in0, scalar=scalar, in1=in1, op0=op0, op1=op1),
                   reads, writes)

        def mm(outs_pairs, reads, writes):
            def fn(pe):
                ins = None
                for out_ap, pairs in outs_pairs:
                    n = len(pairs)
                    for i, (l, r) in enumerate(pairs):
                        ins = pe.matmul(out_ap, l, r, start=(i == 0), stop=(i == n - 1))
                return ins
            T.emit("pe", fn, reads, writes)

        T.dma("sp", cols[:, :], colsd[:, :], writes=(b_const,))
        T.dma("sp", decT[:].rearrange("p a b -> p (a b)"), tabs[:, 0:1024], writes=(b_const,))
        T.dma("sp", xi0[:].rearrange("p a b -> p (a b)"), tabs[:, 1024:2048], writes=(b_const,))
        T.dma("pool", cb[:, :], cbf[:, :], writes=(b_const,))

        def col(c):
            return cols[:, c:c + 1]

        def rmsnorm(gcol0, out_fn, out_bufs):
            bk = bank()
            for g4 in range(4):
                sqv = SQ[:, g4 % 2]
                act(sqv, xT[:, g4 * 4:(g4 + 1) * 4, :], AF.Square, reads=b_x[g4 * 4:(g4 + 1) * 4],
                    writes=(b_sq[g4 % 2],))
                pairs = [(ones, sqv[:, s, :]) for s in range(4)]

                def fn(pe, pairs=pairs, g4=g4, bk=bk):
                    ins = None
                    for i, (l, r) in enumerate(pairs):
                        ins = pe.matmul(psA[:, bk, :], l, r, start=(g4 == 0 and i == 0), stop=(g4 == 3 and i == 3))
                    return ins
                T.emit("pe", fn, reads=(b_sq[g4 % 2], b_const), writes=(b_psA[bk],))
            rsqrt_(rstd[:, :], psA[:, bk, :], 1.0 / D, (b_psA[bk],), (b_rstd,))
            for s in range(16):
                stt(out_fn(s), xT[:, s, :], col(gcol0 + s), rstd[:, :], ALU.mult, ALU.mult,
                    reads=(b_x[s], b_rstd, b_const), writes=(out_bufs[s],))

        def tm_block(slab0, evac):
            slots = [ring_next(("in", slab0 + i)) for i in range(4)]
            s0 = slots[0]
            assert s0 % 4 == 0 and slots == [s0, s0 + 1, s0 + 2, s0 + 3]
            for n in range(NCH):
                bk = bank()
                pairs = [(hT[:, k, n * 128:(n + 1) * 128], ring[:, s0:s0 + 4, k * 128:(k + 1) * 128]) for k in range(16)]
                mm([(psA[:, bk, :], pairs)], reads=b_h + [b_slot[s] for s in slots], writes=(b_psA[bk],))
                evac(n, bk)
            for s_ in slots:
                ring_release(s_)

        def rotary_evac(dst_fn, dst_bufs, cbk):
            t1 = scr(0, [128, 2, 512], F32)
            t2 = scr(4, [128, 2, 512], F32)
            cnt = [0]

            def ev(n, bk):
                i = cnt[0] % 2
                cnt[0] += 1
                ps4 = psA[:, bk, :].rearrange("p (h t i) -> p h t i", h=4, t=2)
                t1v = t1[:, i, :].rearrange("p (h t i) -> p h t i", h=4, t=2)
                t2v = t2[:, i, :].rearrange("p (h t i) -> p h t i", h=4, t=2)
                cosb = rot[:, 0, n, :].unsqueeze(1).unsqueeze(1).to_broadcast([128, 4, 2, 64])
                nsinb = rot[:, 1, n, :].unsqueeze(1).to_broadcast([128, 4, 64])
                sinb = rot[:, 2, n, :].unsqueeze(1).to_broadcast([128, 4, 64])
                tt_(t1v, ps4, cosb, ALU.mult, reads=(b_psA[bk], b_rot), writes=sbufs(0 + 2 * i, 2))
                tt_(t2v[:, :, 0, :], ps4[:, :, 1, :], nsinb, ALU.mult, reads=(b_psA[bk], b_rot), writes=sbufs(4 + 2 * i, 2))
                tt_(t2v[:, :, 1, :], ps4[:, :, 0, :], sinb, ALU.mult, reads=(b_psA[bk], b_rot), writes=sbufs(4 + 2 * i, 2))
                tt_(dst_fn(n, cbk[0]), t1[:, i, :], t2[:, i, :], ALU.add, reads=sbufs(0 + 2 * i, 2) + sbufs(4 + 2 * i, 2),
                    writes=(dst_bufs[n],))
            return ev

        def rg_mm(o):
            s_ = ring_next(("in", SL_RG + o))
            pb = o % 2
            mm([(psB[:, pb * 512:(pb + 1) * 512], [(ring[:, s_, k * 128:(k + 1) * 128], hT[:, k, :]) for k in range(16)])],
               reads=b_h + [b_slot[s_]], writes=(b_psBk[pb],))
            ring_release(s_)

        def rg_evac(o):
            pb = o % 2
            n_, w_, half = o // 4, (o % 4) // 2, o % 2
            dbuf = (b_q if w_ == 0 else b_k)[n_]
            act(QK[:, w_, n_, half * 512:(half + 1) * 512], psB[:, pb * 512:(pb + 1) * 512], AF.Silu,
                reads=(b_psBk[pb],), writes=(dbuf,))

        for tt in tiles:
            T.dma("sp", xT[:], xin[tt], writes=b_x)
            T.dma("sp", rot[:], rott[tt], writes=(b_rot,))
            for lidx, li in enumerate(layers):
              try:
                    cbase = li * C_PER_LAYER
                    T.dma("sp", gbc[:, :], sgug[li].partition_broadcast(128), writes=(b_layer,))
                    T.dma("pool", sgb[:, :], sgub[li], writes=(b_layer,))
                    T.dma("pool", WmT[:].rearrange("p a b -> p (a b)"), sguw[li], writes=(b_layer,))
                    tt_(WmT[:], WmT[:], maskT.unsqueeze(1).to_broadcast([128, 8, 128]), ALU.mult,
                        reads=(b_layer, b_const), writes=(b_layer,))
                    T.dma("pool", pT[:], pin[li, tt], writes=(b_pT,))

                    rmsnorm(cbase + C_MIX, lambda s: hT[:, s, :], b_h)

                    if stop == 'A':
                        raise _Stop()
                    cbk = [0]
                    for blk in range(2):
                        cbk[0] = blk
                        tm_block(SL_K + 4 * blk,
                                 rotary_evac(lambda n, c: QK[:, 1, n, c * 512:(c + 1) * 512], b_k, cbk))
                    for blk in range(4):
                        def ev_rv(n, bk, blk=blk):
                            act(RV[:, n, blk * 512:(blk + 1) * 512], psA[:, bk, :], AF.Copy, reads=(b_psA[bk],), writes=(b_rv[n],))
                        tm_block(SL_RV + 4 * blk, ev_rv)
                    for blk in range(2):
                        cbk[0] = blk
                        tm_block(SL_Q + 4 * blk,
                                 rotary_evac(lambda n, c: QK[:, 0, n, c * 512:(c + 1) * 512], b_q, cbk))

                    if stop == 'B':
                        raise _Stop()
                    kz = scr(8, [128, 2, 1024], BF16)
                    for n in range(NCH):
                        i = n % 2
                        zb = cols[:, C_ZETA:C_ZETA + 8].unsqueeze(2).to_broadcast([128, 8, 128])
                        tt_(kz[:, i, :].rearrange("p (h d) -> p h d", h=8), QK[:, 1, n, :].rearrange("p (h d) -> p h d", h=8),
                            zb, ALU.mult, reads=(b_k[n], b_const), writes=sbufs(8 + 2 * i, 2))
                        bk = bank(4)
                        groups = []
                        for h in range(H):
                            o = psA[:, bk + h // 2, (h % 2) * 256:(h % 2) * 256 + 256]
                            groups.append((o, [(kz[:, i, h * 128:(h + 1) * 128], RV[:, n, h * 256:(h + 1) * 256])]))
                        mm(groups, reads=sbufs(8 + 2 * i, 2) + [b_rv[n]], writes=b_psA[bk:bk + 4])
                        for h in range(H):
                            o = psA[:, bk + h // 2, (h % 2) * 256:(h % 2) * 256 + 256]
                            if n == 0:
                                T.emit("dve", lambda e, o=o, h=h: e.tensor_copy(out=ST[:, h * 256:(h + 1) * 256], in_=o),
                                       reads=(b_psA[bk + h // 2],), writes=(b_st,))
                            else:
                                stt(ST[:, h * 256:(h + 1) * 256], ST[:, h * 256:(h + 1) * 256], col(C_CDN + 8 + h), o,
                                    ALU.mult, ALU.add, reads=(b_psA[bk + h // 2], b_st, b_const), writes=(b_st,))
                        if n < 3:
                            act(PN[:, n, :], ST[:, :], AF.Copy, reads=(b_st,), writes=(b_pn[n],))
                    T.dma("sp", agL_in[tt][li][:, :], ST[:, :], reads=(b_st,), writes=(b_agL[tt][li],))
                    T.wait_all("pool", (b_agL[tt][li],))
                    ccs = T.new_sem("cc")
                    T.engs["pool"].ops.append(
                        lambda eng, a=agL_in[tt][li], b=agL_out[tt][li], ccs=ccs: eng.collective_compute(
                            "AllGather", ALU.bypass, replica_groups=[[0, 1, 2, 3], [4, 5, 6, 7]],
                            ins=[a.ap().opt()], outs=[b.ap().opt()]).then_inc(ccs, 1))
                    T._mark((id(ccs), 1), (), (b_agL[tt][li],))

                    if stop == 'C':
                        raise _Stop()
                    gst = uT[:].rearrange("p a b -> p (a b)").bitcast(F32)
                    gv2 = scr(2, [128, 512], F32)
                    vn = scr(4, [128, 2, 1024], BF16)
                    sS = scr(8, [128, NCH, 1024], BF16)
                    junk = scr(16, [128, 512], F32)
                    for blk in range(2):
                        slots = [ring_next(("in", SL_V + 4 * blk + i)) for i in range(4)]
                        s0 = slots[0]
                        assert s0 % 4 == 0
                        for n in range(NCH):
                            bk = bank()
                            pairs = [(hT[:, k, n * 128:(n + 1) * 128], ring[:, s0:s0 + 4, k * 128:(k + 1) * 128]) for k in range(16)]
                            mm([(psA[:, bk, :], pairs)], reads=b_h + [b_slot[s_] for s_ in slots], writes=(b_psA[bk],))
                            if blk == 0:
                                act(gst[:, n * 512:(n + 1) * 512], psA[:, bk, :], AF.Gelu_apprx_tanh,
                                    reads=(b_psA[bk],), writes=(b_u[2 * n], b_u[2 * n + 1]))
                                act(junk, gst[:, n * 512:(n + 1) * 512], AF.Square,
                                    reads=(b_u[2 * n], b_u[2 * n + 1]), writes=sbufs(16, 2))
                                T.emit("dve", lambda e, n=n: e.tensor_reduce(out=ssq[:, n:n + 1], in_=junk, axis=mybir.AxisListType.X, op=ALU.add),
                                       reads=sbufs(16, 2), writes=(b_ssq,))
                            else:
                                i = n % 2
                                act(gv2, psA[:, bk, :], AF.Gelu_apprx_tanh, reads=(b_psA[bk],), writes=sbufs(2, 2))
                                act(junk, gv2, AF.Square, reads=sbufs(2, 2), writes=sbufs(16, 2))
                                T.emit("dve", lambda e, n=n: e.tensor_reduce(out=ssq[:, 4 + n:5 + n], in_=junk, axis=mybir.AxisListType.X, op=ALU.add),
                                       reads=sbufs(16, 2), writes=(b_ssq,))
                                tt_(ssq[:, n:n + 1], ssq[:, n:n + 1], ssq[:, 4 + n:5 + n], ALU.add, reads=(b_ssq,), writes=(b_ssq,))
                                rsqrt_(ssq[:, n:n + 1], ssq[:, n:n + 1], 1.0 / 1024, (b_ssq,), (b_ssq,))
                                stt(vn[:, i, 0:512], gst[:, n * 512:(n + 1) * 512], ssq[:, n:n + 1], gbc[:, 0:512], ALU.mult, ALU.mult,
                                    reads=(b_u[2 * n], b_u[2 * n + 1], b_ssq, b_layer), writes=sbufs(4 + 2 * i, 2))
                                stt(vn[:, i, 512:1024], gv2, ssq[:, n:n + 1], gbc[:, 512:1024], ALU.mult, ALU.mult,
                                    reads=sbufs(2, 2) + [b_ssq, b_layer], writes=sbufs(4 + 2 * i, 2))
                                bk2 = bank(2)
                                groups = []
                                for g in range(8):
                                    o = psA[:, bk2 + g // 4, (g % 4) * 128:(g % 4) * 128 + 128]
                                    groups.append((o, [(vn[:, i, g * 128:(g + 1) * 128], WmT[:, g, :]),
                                                       (cb[0:1, 0:128], sgb[0:1, g * 128:(g + 1) * 128])]))
                                mm(groups, reads=sbufs(4 + 2 * i, 2) + [b_layer, b_const], writes=b_psA[bk2:bk2 + 2])
                                act(sS[:, n, :], psA[:, bk2:bk2 + 2, :].rearrange("p a b -> p (a b)"), AF.Copy,
                                    reads=b_psA[bk2:bk2 + 2], writes=sbufs(8 + 2 * n, 2))
                        for s_ in slots:
                            ring_release(s_)
                    for g in range(8):
                        s_ = ring_next(("in", SL_U + g))
                        bk = bank()
                        mm([(psA[:, bk, :], [(ring[:, s_, k * 128:(k + 1) * 128], hT[:, k, :]) for k in range(16)])],
                           reads=b_h + [b_slot[s_]], writes=(b_psA[bk],))
                        ring_release(s_)
                        ug = scr(16 + 2 * (g % 2), [128, 512], F32)
                        act(ug, psA[:, bk, :], AF.Gelu_apprx_tanh, reads=(b_psA[bk],), writes=sbufs(16 + 2 * (g % 2), 2))
                        tt_(uT[:, g, :].rearrange("p (n t) -> p n t", n=NCH), ug.rearrange("p (n t) -> p n t", n=NCH),
                            sS[:, :, g * 128:(g + 1) * 128], ALU.mult,
                            reads=sbufs(16 + 2 * (g % 2), 2) + sbufs(8, 8), writes=(b_u[g],))

                    if stop == 'D':
                        raise _Stop()
                    nterm = 3 if tt == 0 else 7
                    T.wait_all("sp", (b_agL[tt][li],))
                    first = True
                    for term in range(nterm):
                        if tt == 0:
                            src = agL_out[0][li][term * 128:(term + 1) * 128, :]
                            sbuf_src = b_agL[0][li]
                        elif term < 4:
                            src = agL_out[0][li][term * 128:(term + 1) * 128, :]
                            sbuf_src = b_agL[0][li]
                        else:
                            src = agL_out[1][li][(term - 4) * 128:(term - 3) * 128, :]
                            sbuf_src = b_agL[1][li]
                        i = term % 2
                        stg = scr(8 * i, [128, 2048], F32)
                        T.dma("sp", stg, src, reads=(sbuf_src,), writes=sbufs(8 * i, 8))
                        for h in range(H):
                            cc = col(C_COEF + (tt * 7 + term) * 8 + h)
                            if first:
                                ts_(ST[:, h * 256:(h + 1) * 256], stg[:, h * 256:(h + 1) * 256], cc, None, ALU.mult, None,
                                    reads=sbufs(8 * i, 8) + [b_const], writes=(b_st,))
                            else:
                                stt(ST[:, h * 256:(h + 1) * 256], stg[:, h * 256:(h + 1) * 256], cc, ST[:, h * 256:(h + 1) * 256],
                                    ALU.mult, ALU.add, reads=sbufs(8 * i, 8) + [b_const, b_st], writes=(b_st,))
                        first = False

                    if stop == 'E':
                        raise _Stop()
                    kTs = scr(0, [128, 8, 128], BF16)
                    qTs = scr(2, [128, 8, 128], BF16)
                    qxT = scr(4, [128, 8, 128], BF16)
                    smS = scr(6, [128, 8, 128], BF16)
                    prv = scr(8, [128, 2048], BF16)
                    ysq = scr(12, [128, 16, 128], BF16)
                    ysm = scr(16, [128, 8, 128], BF16)
                    rsh = scr(18, [128, 8, 128], F32)
                    for n in range(NCH):
                        def fn(pe, n=n):
                            ins = None
                            for w in range(2):
                                for h in range(H):
                                    j = w * 8 + h
                                    ins = pe.matmul(psA[:, j // 4, (j % 4) * 128:(j % 4) * 128 + 128],
                                                    QK[:, w, n, h * 128:(h + 1) * 128], ident, start=True, stop=True)
                            return ins
                        T.emit("pe", fn, reads=(b_q[n], b_k[n], b_const), writes=b_psA[0:4])
                        if stop == 'F0a':
                            raise _Stop()
                        psQ = psA[:, 0:2, :].rearrange("p a b -> p (a b)")
                        psK = psA[:, 2:4, :].rearrange("p a b -> p (a b)")
                        act(qTs[:].rearrange("p a b -> p (a b)"), psQ, AF.Copy, reads=b_psA[0:2], writes=sbufs(2, 2))
                        act(kTs[:].rearrange("p a b -> p (a b)"), psK, AF.Copy, reads=b_psA[2:4], writes=sbufs(0, 2))
                        if stop == 'F0b':
                            raise _Stop()
                        for bb in range(2):
                            tt_(qxT[:].rearrange("p a b -> p (a b)")[:, bb * 512:(bb + 1) * 512],
                                qTs[:].rearrange("p a b -> p (a b)")[:, bb * 512:(bb + 1) * 512],
                                xi0[:].rearrange("p a b -> p (a b)")[:, bb * 512:(bb + 1) * 512], ALU.mult,
                                reads=sbufs(2, 2) + [b_const], writes=sbufs(4, 2))
                        if stop == 'F0c':
                            raise _Stop()
                        for h in range(H):
                            if n == 0:
                                T.emit("dve", lambda e, h=h: e.tensor_copy(out=prv[:, h * 256:(h + 1) * 256], in_=ST[:, h * 256:(h + 1) * 256]),
                                       reads=(b_st,), writes=sbufs(8, 4))
                            else:
                                stt(prv[:, h * 256:(h + 1) * 256], ST[:, h * 256:(h + 1) * 256], col(C_CDN + 8 * n + h),
                                    PN[:, n - 1, h * 256:(h + 1) * 256], ALU.mult, ALU.add,
                                    reads=(b_st, b_pn[n - 1], b_const), writes=sbufs(8, 4))
                        if stop == 'F1':
                            raise _Stop()
                        bs = 4
                        groups = []
                        for h in range(H):
                            o = psA[:, bs + h // 4, (h % 4) * 128:(h % 4) * 128 + 128]
                            groups.append((o, [(kTs[:, h, :], qTs[:, h, :])]))
                        mm(groups, reads=sbufs(0, 4), writes=b_psA[bs:bs + 2])
                        for bb in range(2):
                            tt_(smS[:].rearrange("p a b -> p (a b)")[:, bb * 512:(bb + 1) * 512], psA[:, bs + bb, :],
                                decT[:].rearrange("p a b -> p (a b)")[:, bb * 512:(bb + 1) * 512], ALU.mult,
                                reads=[b_psA[bs + bb], b_const], writes=sbufs(6, 2))
                        if stop == 'F2':
                            raise _Stop()
                        by = 0
                        pa[0] = 0
                        groups = []
                        for h in range(H):
                            for e2 in range(2):
                                sl = 2 * h + e2
                                o = psA[:, by + sl // 4, (sl % 4) * 128:(sl % 4) * 128 + 128]
                                groups.append((o, [(RV[:, n, h * 256 + e2 * 128:h * 256 + e2 * 128 + 128], smS[:, h, :]),
                                                   (prv[:, h * 256 + e2 * 128:h * 256 + e2 * 128 + 128], qxT[:, h, :])]))
                        mm(groups, reads=[b_rv[n]] + sbufs(4, 8), writes=b_psA[by:by + 4])
                        rg_mm(4 * n)
                        rg_mm(4 * n + 1)
                        if stop == 'F3':
                            raise _Stop()
                        psY = psA[:, by:by + 4, :].rearrange("p a b -> p (a b)")
                        for bb in range(2):
                            act(ysq[:].rearrange("p a b -> p (a b)")[:, bb * 1024:(bb + 1) * 1024],
                                psA[:, by + 2 * bb:by + 2 * bb + 2, :].rearrange("p a b -> p (a b)"), AF.Square,
                                reads=b_psA[by + 2 * bb:by + 2 * bb + 2], writes=sbufs(12, 4))
                        ysv = ysq[:].rearrange("p (h e) t -> p h e t", e=2)
                        tt_(ysm[:], ysv[:, :, 0, :], ysv[:, :, 1, :], ALU.add, reads=sbufs(12, 4), writes=sbufs(16, 2))
                        bn = bs
                        ysf = ysm[:].rearrange("p a b -> p (a b)")
                        mm([(psA[:, bn, :], [(ones, ysf[:, 0:512])]), (psA[:, bn + 1, :], [(ones, ysf[:, 512:1024])])],
                           reads=sbufs(16, 2) + [b_const], writes=b_psA[bn:bn + 2])
                        rsf = rsh[:].rearrange("p a b -> p (a b)")
                        rsqrt_(rsf, psA[:, bn:bn + 2, :].rearrange("p a b -> p (a b)"), 1.0 / 256, b_psA[bn:bn + 2], sbufs(18, 4))
                        rg_evac(4 * n)
                        rg_evac(4 * n + 1)
                        rg_mm(4 * n + 2)
                        rg_mm(4 * n + 3)
                        rg_evac(4 * n + 2)
                        rg_evac(4 * n + 3)
                        if stop == 'F4':
                            raise _Stop()
                        for bb in range(4):
                            tt_(RV[:, n, bb * 512:(bb + 1) * 512].rearrange("p (h e t) -> p h e t", h=2, e=2),
                                psA[:, by + bb, :].rearrange("p (h e t) -> p h e t", h=2, e=2),
                                rsh[:, 2 * bb:2 * bb + 2, :].unsqueeze(2).to_broadcast([128, 2, 2, 128]), ALU.mult,
                                reads=[b_psA[by + bb]] + sbufs(18, 4), writes=(b_rv[n],))

                    if stop == 'F':
                        raise _Stop()
                    for o in range(16):
                        n_, w_, half = o // 4, (o % 4) // 2, o % 2
                        dbuf = (b_q if w_ == 0 else b_k)[n_]
                        tgv = QK[:, w_, n_, half * 512:(half + 1) * 512].rearrange("p (n t) -> p n t", n=NCH)
                        yv = RV[:, :, o * 128:(o + 1) * 128]
                        stt(yv, tgv, col(cbase + C_RET + o), yv, ALU.mult, ALU.mult,
                            reads=[dbuf, b_const] + b_rv, writes=b_rv)

                    mT = QK[:].rearrange("p a b c -> p (a b c)").rearrange("p (s t) -> p s t", t=TT)
                    b_m = b_q + b_k
                    wa_slot = None
                    for o in range(16):
                        s = ring_next(("in", SL_GA + o))
                        bk = bank()
                        mm([(psA[:, bk, :], [(ring[:, s, k * 128:(k + 1) * 128], hT[:, k, :]) for k in range(16)])],
                           reads=b_h + [b_slot[s]], writes=(b_psA[bk],))
                        ring_release(s)
                        i = o % 2
                        sg = scr(4 + 2 * i, [128, 512], F32)
                        act(sg, psA[:, bk, :], AF.Sigmoid, reads=(b_psA[bk],), writes=sbufs(4 + 2 * i, 2))
                        if o % 2 == 0:
                            wa_slot = ring_next(("wa", o // 2))
                        bk = bank()
                        wa_v = ring[:, wa_slot, :].rearrange("p (o k m) -> p o k m", o=2, k=8)
                        mm([(psA[:, bk, :], [(wa_v[:, o % 2, k, :], uT[:, k, :]) for k in range(8)])],
                           reads=b_u + [b_slot[wa_slot]], writes=(b_psA[bk],))
                        if o % 2 == 1:
                            ring_release(wa_slot)
                        tt_(sg, sg, psA[:, bk, :], ALU.mult, reads=sbufs(4 + 2 * i, 2) + [b_psA[bk]], writes=sbufs(4 + 2 * i, 2))
                        s = ring_next(("in", SL_GB + o))
                        bk = bank()
                        mm([(psA[:, bk, :], [(ring[:, s, k * 128:(k + 1) * 128], hT[:, k, :]) for k in range(16)])],
                           reads=b_h + [b_slot[s]], writes=(b_psA[bk],))
                        ring_release(s)
                        sg2 = scr(8 + 2 * i, [128, 512], F32)
                        act(sg2, psA[:, bk, :], AF.Sigmoid, reads=(b_psA[bk],), writes=sbufs(8 + 2 * i, 2))
                        s = ring_next(("wb", o))
                        bk = bank()
                        mm([(psA[:, bk, :], [(ring[:, s, k * 128:(k + 1) * 128], RV[:, :, k * 128:(k + 1) * 128]) for k in range(16)])],
                           reads=b_rv + [b_slot[s]], writes=(b_psA[bk],))
                        ring_release(s)
                        tt_(sg2, sg2, psA[:, bk, :], ALU.mult, reads=sbufs(8 + 2 * i, 2) + [b_psA[bk]], writes=sbufs(8 + 2 * i, 2))
                        tt_(mT[:, o, :], sg, sg2, ALU.add, reads=sbufs(4 + 2 * i, 2) + sbufs(8 + 2 * i, 2),
                            writes=b_m[4 * (o // 8):4 * (o // 8) + 4])

                    for o in range(16):
                        s = ring_next(("wout", o))
                        bk = bank()
                        mm([(psA[:, bk, :], [(ring[:, s, k * 128:(k + 1) * 128], mT[:, k, :]) for k in range(16)])],
                           reads=b_m + [b_slot[s]], writes=(b_psA[bk],))
                        ring_release(s)
                        tt_(xT[:, o, :], xT[:, o, :], psA[:, bk, :], ALU.add, reads=(b_x[o], b_psA[bk]), writes=(b_x[o],))

                    rmsnorm(cbase + C_FFN, lambda s: hT[:, s, :], b_h)
                    T.emit("dve", lambda e: e.tensor_copy(out=hsend[:], in_=hT[:, :, TT - 2:TT]), reads=b_h, writes=(b_hsend,))
                    T.dma("sp", agH_in[tt][li][:, :], hsend[:].rearrange("p a b -> p (a b)"), reads=(b_hsend,), writes=(b_agH[tt][li],))
                    T.wait_all("pool", (b_agH[tt][li],))
                    cch = T.new_sem("ch")
                    T.engs["pool"].ops.append(
                        lambda eng, a=agH_in[tt][li], b=agH_out[tt][li], cch=cch: eng.collective_compute(
                            "AllGather", ALU.bypass, replica_groups=[[0, 1, 2, 3], [4, 5, 6, 7]],
                            ins=[a.ap().opt()], outs=[b.ap().opt()]).then_inc(cch, 1))
                    T._mark((id(cch), 1), (), (b_agH[tt][li],))
                    T.wait_all("sp", (b_agH[tt][li],))
                    T.dma("sp", hal[:, 0], agH_out[tt][li][:, :].rearrange("(r p) f -> p r f", p=128),
                          reads=(b_agH[tt][li],), writes=(b_hal,))
                    T.dma("sp", hal[:, 1], agH_out[0][li][:, :].rearrange("(r p) f -> p r f", p=128),
                          reads=(b_agH[0][li],), writes=(b_hal,))
                    for t8 in range(8):
                        src = hal[:, t8 // 4, t8 % 4, :]
                        cc = col(C_SEL + tt * 8 + t8)
                        if t8 == 0:
                            ts_(hacc[:, :], src, cc, None, ALU.mult, None, reads=(b_hal, b_const), writes=(b_hacc,))
                        else:
                            stt(hacc[:, :], src, cc, hacc[:, :], ALU.mult, ALU.add, reads=(b_hal, b_const, b_hacc), writes=(b_hacc,))
                    T.emit("dve", lambda e: e.tensor_copy(out=hh[:].rearrange("p a b -> p (a b)"), in_=hacc[:, :]),
                           reads=(b_hacc,), writes=(b_hh,))

                    aS = scr(0, [128, 2, 512], F32)
                    geS = scr(4, [128, 2, 512], F32)
                    actT = scr(8, [128, 2, FG, 512], BF16)
                    for fg in range(NFG):
                        ia = fg % 2
                        for f in range(FG):
                            ff = fg * FG + f
                            i = ff % 2
                            s_ = ring_next(("gate", ff))
                            bk = bank()
                            mm([(psA[:, bk, :], [(ring[:, s_, k * 128:(k + 1) * 128], hT[:, k, :]) for k in range(16)])],
                               reads=b_h + [b_slot[s_]], writes=(b_psA[bk],))
                            hc = ff % 8
                            hp = psB[:, hc * 2:hc * 2 + 2]
                            mm([(hp, [(ring[:, s_, k * 128:(k + 1) * 128], hh[:, k, :]) for k in range(16)])],
                               reads=[b_hh, b_slot[s_]], writes=(b_psB,))
                            ring_release(s_)
                            cw0, cw1, cw2 = (col(cbase + C_CW + t * NF + ff) for t in range(3))
                            cbb = col(cbase + C_CB + ff)
                            a_ = aS[:, i, :]
                            ab = sbufs(2 * i, 2)
                            act(a_, psA[:, bk, :], AF.Identity, reads=(b_psA[bk], b_const), writes=ab, bias=cbb, scale=cw2)
                            stt(a_[:, 1:512], psA[:, bk, 0:511], cw1, a_[:, 1:512], ALU.mult, ALU.add,
                                reads=[b_psA[bk], b_const] + ab, writes=ab)
                            stt(a_[:, 2:512], psA[:, bk, 0:510], cw0, a_[:, 2:512], ALU.mult, ALU.add,
                                reads=[b_psA[bk], b_const] + ab, writes=ab)
                            stt(a_[:, 0:2], hp, cw0, a_[:, 0:2], ALU.mult, ALU.add,
                                reads=[b_psB, b_const] + ab, writes=ab)
                            stt(a_[:, 0:1], hp[:, 1:2], cw1, a_[:, 0:1], ALU.mult, ALU.add,
                                reads=[b_psB, b_const] + ab, writes=ab)
                            act(geS[:, i, :], a_, AF.Gelu_apprx_tanh, reads=ab, writes=sbufs(4 + 2 * i, 2))
                            s_ = ring_next(("up", ff))
                            bk = bank()
                            mm([(psA[:, bk, :], [(ring[:, s_, k * 128:(k + 1) * 128], hT[:, k, :]) for k in range(16)])],
                               reads=b_h + [b_slot[s_]], writes=(b_psA[bk],))
                            ring_release(s_)
                            tt_(actT[:, ia, f, :], geS[:, i, :], psA[:, bk, :], ALU.mult,
                                reads=sbufs(4 + 2 * i, 2) + [b_psA[bk]], writes=sbufs(8 + 4 * ia, 4))
                        dslots = [ring_next(("down", fg, q)) for q in range(4)]
                        for o in range(16):
                            s_ = dslots[o // 4]
                            dv = ring[:, s_, :].rearrange("p (k c) -> p k c", k=FG)
                            bk = bank()
                            mm([(psA[:, bk, :], [(dv[:, kc, (o % 4) * 128:(o % 4) * 128 + 128], actT[:, ia, kc, :]) for kc in range(FG)])],
                               reads=sbufs(8 + 4 * ia, 4) + [b_slot[s_]], writes=(b_psA[bk],))
                            if o % 4 == 3:
                                ring_release(s_)
                            tt_(xT[:, o, :], xT[:, o, :], psA[:, bk, :], ALU.add, reads=(b_x[o], b_psA[bk]), writes=(b_x[o],))

                    rmsnorm(cbase + C_PLE, lambda s: hT[:, s, :], b_h)
                    pp_slot = None
                    for o in range(16):
                        s = ring_next(("pg", o))
                        bk = bank()
                        mm([(psA[:, bk, :], [(ring[:, s, k * 128:(k + 1) * 128], hT[:, k, :]) for k in range(16)])],
                           reads=b_h + [b_slot[s]], writes=(b_psA[bk],))
                        ring_release(s)
                        i = o % 2
                        sg = scr(4 + 2 * i, [128, 512], F32)
                        act(sg, psA[:, bk, :], AF.Sigmoid, reads=(b_psA[bk],), writes=sbufs(4 + 2 * i, 2))
                        if o % 8 == 0:
                            pp_slot = ring_next(("pp", o // 8))
                        pv = ring[:, pp_slot, :].rearrange("p (o k m) -> p o k m", o=8, k=2)
                        bk = bank()
                        mm([(psA[:, bk, :], [(pv[:, o % 8, k, :], pT[:, k, :]) for k in range(2)])],
                           reads=[b_pT, b_slot[pp_slot]], writes=(b_psA[bk],))
                        if o % 8 == 7:
                            ring_release(pp_slot)
                        tt_(sg, sg, psA[:, bk, :], ALU.mult, reads=sbufs(4 + 2 * i, 2) + [b_psA[bk]], writes=sbufs(4 + 2 * i, 2))
                        tt_(xT[:, o, :], xT[:, o, :], sg, ALU.add, reads=[b_x[o]] + sbufs(4 + 2 * i, 2), writes=(b_x[o],))
                    ring_next(None)
                    ring_next(None)

              except _Stop:
                  pass
            if do_final:
                bk = bank()
                for g4 in range(4):
                    sqv = SQ[:, g4 % 2]
                    act(sqv, xT[:, g4 * 4:(g4 + 1) * 4, :], AF.Square, reads=b_x[g4 * 4:(g4 + 1) * 4], writes=(b_sq[g4 % 2],))
                    pairs = [(ones, sqv[:, s_, :]) for s_ in range(4)]

                    def fn(pe, pairs=pairs, g4=g4, bk=bk):
                        ins = None
                        for i, (l, r) in enumerate(pairs):
                            ins = pe.matmul(psA[:, bk, :], l, r, start=(g4 == 0 and i == 0), stop=(g4 == 3 and i == 3))
                        return ins
                    T.emit("pe", fn, reads=(b_sq[g4 % 2], b_const), writes=(b_psA[bk],))
                rsqrt_(rstd[:, :], psA[:, bk, :], 1.0 / D, (b_psA[bk],), (b_rstd,))
                for s_ in range(16):
                    stt(xT[:, s_, :], xT[:, s_, :], col(C_FINAL + s_), rstd[:, :], ALU.mult, ALU.mult,
                        reads=(b_x[s_], b_rstd, b_const), writes=(b_x[s_],))
            T.dma("sp", yout[tt], xT[:], reads=b_x, writes=(b_out,))
        T.wait_all("sp", (b_out,))
        for t_ in range(2):
            T.wait_all("sp", b_agL[t_] + b_agH[t_])

        with nc.Block() as block:
            @block.tensor
            def _(e):
                for f in T.engs["pe"].ops:
                    f(e)

            @block.scalar
            def _(e):
                for f in T.engs["act"].ops:
                    f(e)

            @block.vector
            def _(e):
                for f in T.engs["dve"].ops:
                    f(e)

            @block.gpsimd
            def _(e):
                for f in T.engs["pool"].ops:
                    f(e)

            @block.sync
            def _(e):
                for f in T.engs["sp"].ops:
                    f(e)
    return nc


def pack_weights(w_in, w_a, w_b, w_out, w_gate, w_up, w_down, pg, pp, layers):
    out = np.empty((len(layers), NREAL, 128, 2048), np.float32)

    def fm(W, o):
        return W[:, o * 128:(o + 1) * 128].reshape(16, 128, 128).transpose(1, 0, 2).reshape(128, 2048)

    for lpos, li in enumerate(layers):
        ri = 0
        for d_ in PLAN:
            if d_ is None:
                continue
            kind = d_[0]
            if kind == "in":
                blk = fm(w_in[li], d_[1])
            elif kind == "wa":
                so = d_[1]
                blk = w_a[li][:, so * 256:(so + 1) * 256].reshape(8, 128, 2, 128).transpose(1, 2, 0, 3).reshape(128, 2048)
            elif kind == "wb":
                blk = fm(w_b[li], d_[1])
            elif kind == "wout":
                blk = fm(w_out[li], d_[1])
            elif kind == "gate":
                blk = fm(w_gate[li], d_[1])
            elif kind == "up":
                blk = fm(w_up[li], d_[1])
            elif kind == "down":
                fg, q = d_[1], d_[2]
                blk = w_down[li][fg * 512:(fg + 1) * 512, q * 512:(q + 1) * 512].reshape(4, 128, 512).transpose(1, 0, 2).reshape(128, 2048)
            elif kind == "pg":
                blk = fm(pg[li], d_[1])
            elif kind == "pp":
                sp = d_[1]
                blk = pp[li][:, sp * 1024:(sp + 1) * 1024].reshape(2, 128, 8, 128).transpose(1, 2, 0, 3).reshape(128, 2048)
            out[lpos, ri] = blk
            ri += 1
    return out


def const_tables():
    lg = np.log1p(-np.exp2(-5.0 - np.arange(H, dtype=np.float64)))
    idx = np.arange(128, dtype=np.float64)
    diff = idx[:, None] - idx[None, :]
    dec = np.where(diff >= 0, np.exp(lg[:, None, None] * np.where(diff >= 0, diff, 0.0)[None]), 0.0)
    scale = 128.0 ** -0.5
    decT = (dec.transpose(2, 0, 1) * scale)
    xi = np.exp(lg[:, None] * (idx[None, :] + 1.0))
    xi_bc = np.broadcast_to(xi[None], (128, H, 128))
    tabs = np.concatenate([decT.reshape(128, 1024), xi_bc.reshape(128, 1024)], axis=1).astype(np.float32)
    zeta = (np.exp(lg[:, None] * (127.0 - idx[None, :])).T * scale)
    cd = np.exp(lg * 128.0)
    return tabs, zeta, cd


def make_in_maps(x, p, mix_norm_g, w_in, sgu_norm_g, sgu_w, sgu_b, ret_norm_g, w_branch_a, w_branch_b,
                 w_out, ffn_norm_g, ffn_w_gate, ffn_w_up, ffn_conv_w, ffn_conv_b, ffn_w_down,
                 ple_norm_g, ple_w_gate, ple_w_proj, final_norm_g, layers=(0, 1, 2, 3)):
    f32 = np.float32
    x = np.asarray(x, f32)
    p = np.asarray(p, f32)
    A = lambda a: np.asarray(a, f32)
    wst = pack_weights(A(w_in), A(w_branch_a), A(w_branch_b), A(w_out), A(ffn_w_gate), A(ffn_w_up),
                       A(ffn_w_down), A(ple_w_gate), A(ple_w_proj), tuple(layers))
    tabs, zeta, cd = const_tables()

    def colsT(v):
        return np.asarray(v, f32).reshape(-1, 128).T

    cols_base = np.zeros((128, C_TOTAL), f32)
    for li in range(NL):
        b = li * C_PER_LAYER
        cols_base[:, b + C_MIX:b + C_MIX + 16] = colsT(mix_norm_g[li])
        cols_base[:, b + C_FFN:b + C_FFN + 16] = colsT(ffn_norm_g[li])
        cols_base[:, b + C_PLE:b + C_PLE + 16] = colsT(ple_norm_g[li])
        cols_base[:, b + C_RET:b + C_RET + 16] = colsT(ret_norm_g[li])
        cw = np.asarray(ffn_conv_w[li], f32)
        for t in range(3):
            cols_base[:, b + C_CW + t * NF:b + C_CW + (t + 1) * NF] = colsT(cw[t])
        cols_base[:, b + C_CB:b + C_CB + NF] = colsT(ffn_conv_b[li])
    cols_base[:, C_FINAL:C_FINAL + 16] = colsT(final_norm_g)
    cols_base[:, C_EPS] = EPS
    cols_base[:, C_ZETA:C_ZETA + 8] = zeta.astype(f32)
    for n in range(4):
        cols_base[:, C_CDN + 8 * n:C_CDN + 8 * n + 8] = (cd ** n).astype(f32)[None, :]

    sgug = np.asarray(sgu_norm_g, f32).reshape(NL, 1, 1024)
    sgub = np.asarray(sgu_b, f32).reshape(NL, 1, 1024)
    sguw = np.ascontiguousarray(np.asarray(sgu_w, f32).transpose(0, 3, 1, 2)).reshape(NL, 128, 1024)
    cbf = np.zeros((128, 384), f32)
    cbf[:, 0:128] = 1.0
    cbf[:, 128:256] = np.eye(128, dtype=f32)
    ii = np.arange(128)
    cbf[:, 256:384] = (ii[None, :] >= ii[:, None]).astype(f32)

    inv = np.power(f32(10000.0), -(np.arange(64, dtype=f32) / f32(64)))
    in_maps = []
    for c in range(8):
        b, j = c // 4, c % 4
        xin = np.empty((2, 128, 16, TT), f32)
        pin = np.empty((NL, 2, 128, 2, TT), f32)
        rott = np.empty((2, 128, 3, NCH, 64), f32)
        cols_c = cols_base.copy()
        for tt in range(2):
            g = 4 * tt + j
            xs = x[b, g * TT:(g + 1) * TT, :]
            xin[tt] = xs.T.reshape(16, 128, TT).transpose(1, 0, 2)
            for li in range(NL):
                ps_ = p[li, b, g * TT:(g + 1) * TT, :]
                pin[li, tt] = ps_.T.reshape(2, 128, TT).transpose(1, 0, 2)
            pos = (g * TT + np.arange(TT)).astype(f32)
            ang = (pos[:, None] * inv[None, :]).astype(f32)
            cs = np.cos(ang).astype(f32).reshape(NCH, 128, 64).transpose(1, 0, 2)
            sn = np.sin(ang).astype(f32).reshape(NCH, 128, 64).transpose(1, 0, 2)
            rott[tt, :, 0] = cs
            rott[tt, :, 1] = -sn
            rott[tt, :, 2] = sn
            for term in range(7):
                if tt == 0:
                    cf = cd ** (4 * (j - 1 - term)) if term < min(j, 3) and term < 3 else np.zeros(H)
                    if term >= 3:
                        cf = np.zeros(H)
                else:
                    if term < 4:
                        cf = cd ** (4 * (j + 3 - term))
                    else:
                        r = term - 4
                        cf = cd ** (4 * (j - 1 - r)) if r < j else np.zeros(H)
                cols_c[:, C_COEF + (tt * 7 + term) * 8:C_COEF + (tt * 7 + term) * 8 + 8] = cf.astype(f32)[None, :]
            sel = np.zeros(8, f32)
            if j > 0:
                sel[j - 1] = 1.0
            elif tt == 1:
                sel[4 + 3] = 1.0
            cols_c[:, C_SEL + tt * 8:C_SEL + tt * 8 + 8] = sel[None, :]
        in_maps.append({"xin": xin, "pin": pin, "wst": wst, "cols": cols_c, "sgug": sgug, "sgub": sgub,
                        "sguw": sguw, "tabs": tabs, "rott": rott, "cbf": cbf})

    return in_maps


def kernel(**inputs):
    f32 = np.float32
    in_maps = make_in_maps(layers=LAYERS, **inputs)
    nc = build_program(layers=LAYERS)
    res = run_bass_kernel_spmd(nc, in_maps, core_ids=list(range(8)))
    out = np.empty((2, 4096, D), f32)
    for c in range(8):
        b, j = c // 4, c % 4
        y = np.asarray(res.results[c]["yout"])
        for tt in range(2):
            g = 4 * tt + j
            out[b, g * TT:(g + 1) * TT, :] = y[tt].transpose(1, 0, 2).reshape(D, TT).T
    return out
```
